# Optimizing a Trainium2 kernel written in Bass

```python
import math
import jax, jax.numpy as jnp
from jax import lax
import numpy as np

D_MODEL = 1024
BATCH = 4
SEQ = 8192
DEPTH = 4

SSM_WIDTH = D_MODEL // 2
SSM_GROUP = 16
SSM_GROUPS = SSM_WIDTH // SSM_GROUP
SSM_STATE = 64
MLA_HEADS = 8
QK_NOPE = 64
QK_ROPE = 32
QK_HEAD = QK_NOPE + QK_ROPE
V_HEAD = 64
Q_LORA = 384
KV_LORA = 256
MLA_WIDTH = MLA_HEADS * V_HEAD
ROPE_THETA = 10000.0
Q_BLOCK = 128
D_FF = 4 * D_MODEL
EPS = 1e-6
IN_SIZES = (SSM_WIDTH, Q_LORA, KV_LORA, QK_ROPE, 2 * D_MODEL)
IN_COLS = sum(IN_SIZES)
IN_SPLITS = [int(v) for v in np.cumsum(IN_SIZES)[:-1]]

kernel_name = "hybrid_s5_mla_encoder"


def rms_norm(x, g):
    xf = x.astype(jnp.float32)
    y = xf * lax.rsqrt(jnp.mean(xf * xf, axis=-1, keepdims=True) + EPS)
    return (y * g.astype(jnp.float32)).astype(x.dtype)


def rope_tables(seq, dtype):
    half = QK_ROPE // 2
    inv_freq = ROPE_THETA ** (-jnp.arange(half, dtype=jnp.float32) / half)
    ang = jnp.arange(seq, dtype=jnp.float32)[:, None] * inv_freq[None, :]
    return jnp.cos(ang).astype(dtype), jnp.sin(ang).astype(dtype)


def apply_rope(x, cos, sin):
    x1, x2 = jnp.split(x, 2, axis=-1)
    return jnp.concatenate([x1 * cos - x2 * sin, x1 * sin + x2 * cos], axis=-1)


def _affine_combine(e1, e2):
    a1r, a1i, b1r, b1i = e1
    a2r, a2i, b2r, b2i = e2
    ar = a2r * a1r - a2i * a1i
    ai = a2r * a1i + a2i * a1r
    br = a2r * b1r - a2i * b1i + b2r
    bi = a2r * b1i + a2i * b1r + b2i
    return ar, ai, br, bi


def zoh_discretise(lam_re, lam_im, log_step, b_re, b_im):
    f32 = jnp.float32
    lam_re = lam_re.astype(f32); lam_im = lam_im.astype(f32)
    step = jnp.exp(log_step.astype(f32))[:, None]
    mag = jnp.exp(lam_re * step)
    abar_r = mag * jnp.cos(lam_im * step)
    abar_i = mag * jnp.sin(lam_im * step)
    nr = abar_r - 1.0
    ni = abar_i
    den = lam_re * lam_re + lam_im * lam_im
    fr = (nr * lam_re + ni * lam_im) / den
    fi = (ni * lam_re - nr * lam_im) / den
    b_re = b_re.astype(f32); b_im = b_im.astype(f32)
    bbar_r = fr[..., None] * b_re - fi[..., None] * b_im
    bbar_i = fr[..., None] * b_im + fi[..., None] * b_re
    return abar_r, abar_i, bbar_r, bbar_i


def s5_states(u, lam_re, lam_im, log_step, b_re, b_im, reverse):
    abar_r, abar_i, bbar_r, bbar_i = zoh_discretise(lam_re, lam_im, log_step, b_re, b_im)
    bu_r = jnp.einsum('bsgp,gnp->bsgn', u, bbar_r)
    bu_i = jnp.einsum('bsgp,gnp->bsgn', u, bbar_i)
    a_r = jnp.broadcast_to(abar_r, bu_r.shape)
    a_i = jnp.broadcast_to(abar_i, bu_i.shape)
    _, _, xr, xi = lax.associative_scan(_affine_combine, (a_r, a_i, bu_r, bu_i), reverse=reverse, axis=1)
    return xr, xi


def s5_branch(u, lam_re, lam_im, log_step, b_re, b_im, c_re, c_im, d, w_glu, b_glu):
    bsz, seq, _ = u.shape
    uf = u.astype(jnp.float32).reshape(bsz, seq, SSM_GROUPS, SSM_GROUP)
    xr_f, xi_f = s5_states(uf, lam_re[0], lam_im[0], log_step[0], b_re[0], b_im[0], reverse=False)
    xr_b, xi_b = s5_states(uf, lam_re[1], lam_im[1], log_step[1], b_re[1], b_im[1], reverse=True)
    xr = xr_f + xr_b
    xi = xi_f + xi_b
    y = (jnp.einsum('bsgn,gpn->bsgp', xr, c_re.astype(jnp.float32))
         - jnp.einsum('bsgn,gpn->bsgp', xi, c_im.astype(jnp.float32))
         + d.astype(jnp.float32) * uf)
    y = y.reshape(bsz, seq, SSM_WIDTH).astype(u.dtype)
    y = jax.nn.gelu(y)
    return y * jax.nn.sigmoid(y @ w_glu + b_glu)


def mla_branch(cq, ckv, k_rope, q_norm_g, kv_norm_g, w_q_up, w_kv_up, q_head_g, k_head_g, cos, sin):
    bsz, seq, _ = cq.shape
    q = (rms_norm(cq, q_norm_g) @ w_q_up).reshape(bsz, seq, MLA_HEADS, QK_HEAD)
    kv = (rms_norm(ckv, kv_norm_g) @ w_kv_up).reshape(bsz, seq, MLA_HEADS, QK_NOPE + V_HEAD)
    k_nope, v = kv[..., :QK_NOPE], kv[..., QK_NOPE:]
    k = jnp.concatenate([k_nope, jnp.broadcast_to(k_rope[:, :, None, :], (bsz, seq, MLA_HEADS, QK_ROPE))], axis=-1)
    q = rms_norm(q, q_head_g)
    k = rms_norm(k, k_head_g)
    c4, s4 = cos[None, :, None, :], sin[None, :, None, :]
    q = jnp.concatenate([q[..., :QK_NOPE], apply_rope(q[..., QK_NOPE:], c4, s4)], axis=-1)
    k = jnp.concatenate([k[..., :QK_NOPE], apply_rope(k[..., QK_NOPE:], c4, s4)], axis=-1)
    q = q * (QK_HEAD ** -0.5)
    n_blk = seq // Q_BLOCK
    qb = q.reshape(bsz, n_blk, Q_BLOCK, MLA_HEADS, QK_HEAD).transpose(1, 0, 2, 3, 4)

    def attend(q_blk):
        s = jnp.einsum('bqhd,bkhd->bhqk', q_blk, k, preferred_element_type=jnp.float32)
        p = jax.nn.softmax(s, axis=-1)
        return jnp.einsum('bhqk,bkhd->bqhd', p.astype(v.dtype), v)

    o = lax.map(attend, qb)
    return o.transpose(1, 0, 2, 3, 4).reshape(bsz, seq, MLA_WIDTH)


def setup_inputs(seed: int = 0) -> dict:
    key = jax.random.key(seed)
    ks = jax.random.split(key, 32)
    f32 = jnp.float32
    G, N, P = SSM_GROUPS, SSM_STATE, SSM_GROUP

    def nrm(k, shape, scale):
        return jax.random.normal(k, shape, f32) * scale

    def gain(k, shape):
        return 1.0 + 0.05 * jax.random.normal(k, shape, f32)

    n_idx = jnp.arange(N, dtype=f32)
    lam_re = -0.5 + 0.01 * jax.random.normal(ks[4], (DEPTH, 2, G, N), f32)
    lam_im = math.pi * n_idx + 0.01 * jax.random.normal(ks[5], (DEPTH, 2, G, N), f32)
    log_step = jax.random.uniform(ks[6], (DEPTH, 2, G), f32, math.log(1e-3), math.log(1e-1))
    b_scale = (1.0 / math.sqrt(P)) / math.sqrt(2.0)
    c_scale = (1.0 / math.sqrt(N)) / math.sqrt(2.0)
    return {
        "x": jax.random.normal(ks[0], (BATCH, SEQ, D_MODEL), f32),
        "mix_norm_g": gain(ks[1], (DEPTH, D_MODEL)),
        "w_in": nrm(ks[2], (DEPTH, D_MODEL, IN_COLS), D_MODEL ** -0.5),
        "b_gate": nrm(ks[3], (DEPTH, 2, D_MODEL), 0.02),
        "ssm_lam_re": lam_re,
        "ssm_lam_im": lam_im,
        "ssm_log_step": log_step,
        "ssm_b_re": nrm(ks[7], (DEPTH, 2, G, N, P), b_scale),
        "ssm_b_im": nrm(ks[8], (DEPTH, 2, G, N, P), b_scale),
        "ssm_c_re": nrm(ks[9], (DEPTH, G, P, N), c_scale),
        "ssm_c_im": nrm(ks[10], (DEPTH, G, P, N), c_scale),
        "ssm_d": nrm(ks[11], (DEPTH, G, P), 1.0),
        "w_glu": nrm(ks[12], (DEPTH, SSM_WIDTH, SSM_WIDTH), SSM_WIDTH ** -0.5),
        "b_glu": nrm(ks[13], (DEPTH, SSM_WIDTH), 0.02),
        "w_out_ssm": nrm(ks[14], (DEPTH, SSM_WIDTH, D_MODEL), SSM_WIDTH ** -0.5),
        "q_norm_g": gain(ks[15], (DEPTH, Q_LORA)),
        "kv_norm_g": gain(ks[16], (DEPTH, KV_LORA)),
        "w_q_up": nrm(ks[17], (DEPTH, Q_LORA, MLA_HEADS * QK_HEAD), Q_LORA ** -0.5),
        "w_kv_up": nrm(ks[18], (DEPTH, KV_LORA, MLA_HEADS * (QK_NOPE + V_HEAD)), KV_LORA ** -0.5),
        "q_head_g": gain(ks[19], (DEPTH, QK_HEAD)),
        "k_head_g": gain(ks[20], (DEPTH, QK_HEAD)),
        "w_out_mla": nrm(ks[21], (DEPTH, MLA_WIDTH, D_MODEL), MLA_WIDTH ** -0.5),
        "w_o": nrm(ks[22], (DEPTH, D_MODEL, D_MODEL), D_MODEL ** -0.5),
        "ffn_norm_g": gain(ks[23], (DEPTH, D_MODEL)),
        "w_ff1": nrm(ks[24], (DEPTH, D_MODEL, D_FF), D_MODEL ** -0.5),
        "w_ff2": nrm(ks[25], (DEPTH, D_FF, D_MODEL), D_FF ** -0.5),
    }


def reference(x, mix_norm_g, w_in, b_gate, ssm_lam_re, ssm_lam_im, ssm_log_step, ssm_b_re, ssm_b_im,
              ssm_c_re, ssm_c_im, ssm_d, w_glu, b_glu, w_out_ssm, q_norm_g, kv_norm_g, w_q_up, w_kv_up,
              q_head_g, k_head_g, w_out_mla, w_o, ffn_norm_g, w_ff1, w_ff2):
    bsz, seq, _ = x.shape
    cos, sin = rope_tables(seq, x.dtype)
    for l in range(DEPTH):
        h = rms_norm(x, mix_norm_g[l])
        proj = h @ w_in[l]
        u_ssm, cq, ckv, k_rope, gate_pre = jnp.split(proj, IN_SPLITS, axis=-1)
        gates = jax.nn.sigmoid(gate_pre.reshape(bsz, seq, 2, D_MODEL) + b_gate[l])
        y_ssm = s5_branch(u_ssm, ssm_lam_re[l], ssm_lam_im[l], ssm_log_step[l], ssm_b_re[l], ssm_b_im[l],
                          ssm_c_re[l], ssm_c_im[l], ssm_d[l], w_glu[l], b_glu[l])
        y_mla = mla_branch(cq, ckv, k_rope, q_norm_g[l], kv_norm_g[l], w_q_up[l], w_kv_up[l],
                           q_head_g[l], k_head_g[l], cos, sin)
        merged = gates[:, :, 0, :] * (y_ssm @ w_out_ssm[l]) + gates[:, :, 1, :] * (y_mla @ w_out_mla[l])
        x = x + merged @ w_o[l]
        h = rms_norm(x, ffn_norm_g[l])
        x = x + jnp.square(jax.nn.relu(h @ w_ff1[l])) @ w_ff2[l]
    return x
```

```python
import math
import os
KSTOP = int(os.environ.get('KSTOP', '99'))
CSPLIT = int(os.environ.get('CSPLIT', '0'))
SCAN_ENG1 = os.environ.get('SCAN_ENG1', 'dve')
from contextlib import ExitStack
import numpy as np
import ml_dtypes
import concourse.bass as bass
import concourse.mybir as mybir
from concourse.bass_utils import run_bass_kernel_spmd

F32 = mybir.dt.float32
BF16 = mybir.dt.bfloat16
I32 = mybir.dt.int32
AF = mybir.ActivationFunctionType
ALU = mybir.AluOpType
AX = mybir.AxisListType

D = 1024
DEPTH = 4
SW = 512
G = 32
NS = 64
P16 = 16
H = 8
QK = 96
NOPE = 64
ROPE = 32
VH = 64
QL = 384
KVL = 256
DFF = 4096
INC = 3232
EPS = 1e-6
NSLOT = 12


class Buf:
    __slots__ = ("t", "lw", "rd", "psum")

    def __init__(self, t, psum=False):
        self.t = t
        self.psum = psum
        self.lw = None
        self.rd = {}


class Prog:
    def __init__(self, nc):
        self.nc = nc
        self.eng = {"pe": nc.tensor, "act": nc.scalar, "dve": nc.vector, "pool": nc.gpsimd, "sp": nc.sync}
        self.sem = {}
        self.cnt = {}
        for k in ["pe", "act", "dve", "pool"]:
            self.sem[k] = nc.alloc_semaphore("s_" + k)
            self.cnt[k] = 0
        for i in range(NSLOT):
            k = "d%d" % i
            self.sem[k] = nc.alloc_semaphore("s_" + k)
            self.cnt[k] = 0
        self.seen = {e: {} for e in self.eng}
        self.ndma = 0
        self.stack = None

    def _wait(self, e, tok):
        k, v = tok
        if self.seen[e].get(k, 0) >= v:
            return
        self.seen[e][k] = v
        self.eng[e].wait_ge(self.sem[k], v)

    def _deps(self, e, reads, writes):
        for b in reads:
            if b.lw is not None:
                self._wait(e, b.lw)
            if b.psum:
                for k, v in b.rd.items():
                    if k != e:
                        self._wait(e, (k, v))
        for b in writes:
            if b.lw is not None and not (b.lw[0] == e and e == "pe"):
                self._wait(e, b.lw)
            for k, v in b.rd.items():
                if k == e:
                    continue
                self._wait(e, (k, v))

    def _mark(self, tok, reads, writes):
        for b in reads:
            b.rd[tok[0]] = max(b.rd.get(tok[0], 0), tok[1])
        for b in writes:
            b.lw = tok
            b.rd = {}

    def op(self, e, fn, reads=(), writes=()):
        self._deps(e, reads, writes)
        ins = fn(self.eng[e])
        self.cnt[e] += 1
        ins.then_inc(self.sem[e], 1)
        tok = (e, self.cnt[e])
        self.seen[e][e] = max(self.seen[e].get(e, 0), 0)
        self._mark(tok, reads, writes)
        return tok

    def dma(self, e, out, in_, reads=(), writes=(), **kw):
        self._deps(e, reads, writes)
        k = "d%d" % (self.ndma % NSLOT)
        self.ndma += 1
        if self.cnt[k] > 0:
            self._wait(e, (k, self.cnt[k]))
        ins = self.eng[e].dma_start(out=out, in_=in_, **kw)
        self.cnt[k] += 16
        ins.then_inc(self.sem[k], 16)
        tok = (k, self.cnt[k])
        self._mark(tok, reads, writes)
        return tok

    def barrier(self):
        for e in self.eng:
            for k in self.sem:
                if k != e and self.cnt[k] > 0:
                    self._wait(e, (k, self.cnt[k]))

    def sb(self, es, name, shape, dt):
        self.nname = getattr(self, "nname", 0) + 1
        return Buf(es.enter_context(self.nc.sbuf_tensor("%s_%d" % (name, self.nname), list(shape), dt)))


def bcast_row(ap_1d, parts):
    return ap_1d.unsqueeze(0).partition_broadcast(parts) if hasattr(ap_1d, "partition_broadcast") else ap_1d


def build(T, nlayers, debug=False, phases="abcdf"):
    nc = bass.Bass("TRN2", target_bir_lowering=False)
    NB = T // 512
    NT = T // 128

    def din(name, shape, dt=F32):
        return nc.dram_tensor(name, list(shape), dt, kind="ExternalInput").ap()

    def dscr(name, shape, dt):
        kind = "ExternalOutput" if debug else "Internal"
        return nc.dram_tensor(name, list(shape), dt, kind=kind).ap()

    x_in = din("x", [T, D])
    w = {}
    shapes = {
        "mix_norm_g": [DEPTH, D], "w_in": [DEPTH, D, INC], "b_gate": [DEPTH, 2, D],
        "ssm_lam_re": [DEPTH, 2, G, NS], "ssm_lam_im": [DEPTH, 2, G, NS], "ssm_log_step": [DEPTH, 2, G],
        "ssm_b_re": [DEPTH, 2, G, NS, P16], "ssm_b_im": [DEPTH, 2, G, NS, P16],
        "ssm_c_re": [DEPTH, G, P16, NS], "ssm_c_im": [DEPTH, G, P16, NS], "ssm_d": [DEPTH, G, P16],
        "w_glu": [DEPTH, SW, SW], "b_glu": [DEPTH, SW], "w_out_ssm": [DEPTH, SW, D],
        "q_norm_g": [DEPTH, QL], "kv_norm_g": [DEPTH, KVL], "w_q_up": [DEPTH, QL, H * QK],
        "w_kv_up": [DEPTH, KVL, H * 128], "q_head_g": [DEPTH, QK], "k_head_g": [DEPTH, QK],
        "w_out_mla": [DEPTH, SW, D], "w_o": [DEPTH, D, D], "ffn_norm_g": [DEPTH, D],
        "w_ff1": [DEPTH, D, DFF], "w_ff2": [DEPTH, DFF, D],
    }
    for k, s in shapes.items():
        w[k] = din(k, s)
    ident_f = din("ident_f", [128, 128])
    ident_b = din("ident_b", [128, 128], BF16)
    y_out = nc.dram_tensor("y", [T, D], F32, kind="ExternalOutput").ap()
    xs = dscr("xs", [T, D], F32)
    x1 = dscr("x1", [T, D], F32)
    uT = dscr("uT", [SW, T], BF16)
    qT = dscr("qT", [H, QK, T], BF16)
    kT = dscr("kT", [H, QK, T], BF16)
    vA = dscr("vA", [T, H, VH + 1], BF16)
    gT = dscr("gT", [2 * D, T], BF16)
    ysT = dscr("ysT", [SW, T], BF16)
    ymT = dscr("ymT", [SW, T], BF16)
    cs_tab = din("cs_tab", [2, QK, T])
    ones_d = din("ones_b", [128, 128], BF16)
    einj_d = din("einj", [32, 2, QK], BF16)
    sel_d = din("sel", [VH + 1, VH])
    bmask_d = din("bmask", [128, 128])
    ygT = dscr("ygT", [SW, T], BF16)

    pg = Prog(nc)
    ps = []
    psT = []
    pctr = [0]

    def alloc_ps(es):
        pctr[0] += 1
        ps[:] = [Buf(es.enter_context(nc.psum_tensor("psb%d_%d" % (pctr[0], i), [128, 512], F32)), True) for i in range(6)]
        psT[:] = [Buf(es.enter_context(nc.psum_tensor("pst%d_%d" % (pctr[0], i), [128, 1024], BF16)), True) for i in range(2)]

    with ExitStack() as g_es:
        idb = pg.sb(g_es, "idb", [128, 128], BF16)
        idf = pg.sb(g_es, "idf", [128, 128], F32)
        epsc = pg.sb(g_es, "epsc", [128, 1], F32)
        pg.dma("sp", idb.t[:, :], ident_b, writes=[idb])
        pg.dma("sp", idf.t[:, :], ident_f, writes=[idf])
        pg.op("dve", lambda e: e.memset(epsc.t[:, :], EPS), writes=[epsc])
        ones = pg.sb(g_es, "ones", [128, 128], BF16)
        einj = pg.sb(g_es, "einj", [32, 2, QK], BF16)
        pg.dma("sp", ones.t[:, :], ones_d, writes=[ones])
        pg.dma("sp", einj.t[:, :, :], einj_d, writes=[einj])

        def load_weight_bf16(es, name, src, K, N, gain_src=None, chunk=2048, stg=None):
            kt = K // 128
            wt = pg.sb(es, name, [128, kt, N], BF16)
            gcol = None
            if gain_src is not None:
                gcol = pg.sb(es, name + "_g", [128, kt], F32)
                pg.dma("pool", gcol.t[:, :], gain_src.rearrange("(k p) -> p k", p=128), writes=[gcol],
                       allow_slow_non_contiguous=True)
            if stg is None:
                stg = [pg.sb(es, name + "_s%d" % i, [128, chunk], F32) for i in range(2)]
            n = 0
            for k in range(kt):
                for c0 in range(0, N, chunk):
                    cw = min(chunk, N - c0)
                    s = stg[n % 2]
                    pg.dma("sp" if n % 2 == 0 else "pool", s.t[:, 0:cw], src[k * 128:(k + 1) * 128, c0:c0 + cw],
                           writes=[s])
                    eng = "act" if n % 2 == 0 else "dve"
                    if gcol is None:
                        if eng == "act":
                            pg.op("act", lambda e, s=s, k=k, c0=c0, cw=cw: e.copy(wt.t[:, k, c0:c0 + cw], s.t[:, 0:cw]),
                                  reads=[s], writes=[wt])
                        else:
                            pg.op("dve", lambda e, s=s, k=k, c0=c0, cw=cw: e.tensor_copy(wt.t[:, k, c0:c0 + cw], s.t[:, 0:cw]),
                                  reads=[s], writes=[wt])
                    else:
                        pg.op("dve", lambda e, s=s, k=k, c0=c0, cw=cw: e.tensor_scalar(
                            wt.t[:, k, c0:c0 + cw], s.t[:, 0:cw], gcol.t[:, k:k + 1], None, ALU.mult),
                            reads=[s, gcol], writes=[wt])
                    n += 1
            return wt

        def rms_rows(es_bufs, xt, width, out_bf):
            junk, ss = es_bufs
            pg.op("act", lambda e: e.activation(junk.t[:, 0:width], xt.t[:, 0:width], AF.Square, accum_out=ss.t[:, 0:1]),
                  reads=[xt], writes=[junk, ss])
            pg.op("act", lambda e: e.activation(ss.t[:, 0:1], ss.t[:, 0:1], AF.Sqrt, bias=epsc.t[:, 0:1], scale=1.0 / width),
                  reads=[ss, epsc], writes=[ss])
            pg.op("dve", lambda e: e.reciprocal(ss.t[:, 0:1], ss.t[:, 0:1]), reads=[ss], writes=[ss])
            pg.op("dve", lambda e: e.tensor_scalar(out_bf.t[:, 0:width], xt.t[:, 0:width], ss.t[:, 0:1], None, ALU.mult),
                  reads=[xt, ss], writes=[out_bf])

        def phase_ffn(l, src, dst):
            for half in range(2):
                with ExitStack() as es:
                    alloc_ps(es)
                    hs = DFF // 2
                    nk = hs // 128
                    stg_f = [pg.sb(es, "stgf%d" % i, [128, 2048], F32) for i in range(2)]
                    w1 = load_weight_bf16(es, "w1", w["w_ff1"][l][:, half * hs:(half + 1) * hs], D, hs,
                                          gain_src=w["ffn_norm_g"][l], stg=stg_f)
                    w2 = load_weight_bf16(es, "w2", w["w_ff2"][l][half * hs:(half + 1) * hs, :], hs, D, chunk=1024, stg=stg_f)
                    xt = [pg.sb(es, "xt%d" % i, [128, 4, D], F32) for i in range(2)]
                    xa = [pg.sb(es, "xa%d" % i, [128, 4, D], F32) for i in range(2)] if half == 1 else xt
                    xn = pg.sb(es, "xn", [128, D], BF16)
                    junk = pg.sb(es, "junk", [128, D], BF16)
                    ss = pg.sb(es, "ss", [128, 1], F32)
                    hTs = [pg.sb(es, "hT%d" % i, [128, 8, 512], BF16) for i in range(2)]
                    hid = pg.sb(es, "hid", [128, nk, 512], BF16)
                    rl = [pg.sb(es, "rl%d" % i, [128, 512], F32) for i in range(2)]

                    def prep(tb):
                        xb, hT = xt[tb % 2], hTs[tb % 2]
                        pg.dma("sp", xb.t[:, :, :], src[tb * 512:(tb + 1) * 512, :].rearrange("(t p) d -> p t d", p=128), writes=[xb])
                        if half == 1:
                            pg.dma("pool", xa[tb % 2].t[:, :, :], dst[tb * 512:(tb + 1) * 512, :].rearrange("(t p) d -> p t d", p=128),
                                   writes=[xa[tb % 2]])
                        for tt in range(4):
                            pg.op("act", lambda e, tt=tt: e.activation(junk.t[:, :], xb.t[:, tt, :], AF.Square, accum_out=ss.t[:, 0:1]),
                                  reads=[xb], writes=[junk, ss])
                            pg.op("act", lambda e: e.activation(ss.t[:, 0:1], ss.t[:, 0:1], AF.Sqrt, bias=epsc.t[:, 0:1], scale=1.0 / D),
                                  reads=[ss, epsc], writes=[ss])
                            pg.op("dve", lambda e: e.reciprocal(ss.t[:, 0:1], ss.t[:, 0:1]), reads=[ss], writes=[ss])
                            pg.op("dve", lambda e, tt=tt: e.tensor_scalar(xn.t[:, :], xb.t[:, tt, :], ss.t[:, 0:1], None, ALU.mult),
                                  reads=[xb, ss], writes=[xn])
                            pt = psT[tt % 2]
                            for k in range(8):
                                pg.op("pe", lambda e, k=k, pt=pt: e.transpose(pt.t[:, k * 128:(k + 1) * 128], xn.t[:, k * 128:(k + 1) * 128], idb.t[:, :]),
                                      reads=[xn, idb], writes=[pt])
                            pg.op("dve", lambda e, tt=tt, pt=pt: e.tensor_copy(hT.t[:, :, tt * 128:(tt + 1) * 128],
                                                                              pt.t[:, :].rearrange("p (k t) -> p k t", k=8)),
                                  reads=[pt], writes=[hT])

                    prep(0)
                    for tb in range(NB):
                        hT, xo = hTs[tb % 2], xa[tb % 2]
                        for ft in range(nk):
                            pb = ps[ft % 4]
                            for k in range(8):
                                pg.op("pe", lambda e, k=k, ft=ft, pb=pb: e.matmul(pb.t[:, :], w1.t[:, k, ft * 128:(ft + 1) * 128], hT.t[:, k, :],
                                                                               start=(k == 0), stop=(k == 7)),
                                      reads=[w1, hT], writes=[pb])
                            r = rl[ft % 2]
                            pg.op("act", lambda e, pb=pb, r=r: e.activation(r.t[:, :], pb.t[:, :], AF.Relu), reads=[pb], writes=[r])
                            pg.op("pool", lambda e, r=r, ft=ft: e.tensor_tensor(hid.t[:, ft, :], r.t[:, :], r.t[:, :], ALU.mult),
                                  reads=[r], writes=[hid])
                        if tb + 1 < NB:
                            prep(tb + 1)
                        for tt in range(4):
                            for ch in range(2):
                                pb = ps[4 + ch]
                                for k in range(nk):
                                    pg.op("pe", lambda e, k=k, tt=tt, ch=ch, pb=pb: e.matmul(
                                        pb.t[:, :], hid.t[:, k, tt * 128:(tt + 1) * 128], w2.t[:, k, ch * 512:(ch + 1) * 512],
                                        start=(k == 0), stop=(k == nk - 1)), reads=[hid, w2], writes=[pb])
                                pg.op("dve", lambda e, tt=tt, ch=ch, pb=pb: e.tensor_tensor(
                                    xo.t[:, tt, ch * 512:(ch + 1) * 512], pb.t[:, :], xo.t[:, tt, ch * 512:(ch + 1) * 512], ALU.add),
                                    reads=[pb, xo], writes=[xo])
                        pg.dma("sp", dst[tb * 512:(tb + 1) * 512, :].rearrange("(t p) d -> p t d", p=128), xo.t[:, :, :], reads=[xo])
                pg.barrier()

        def phase_a(l, src):
            with ExitStack() as es:
                alloc_ps(es)
                stg_a = [pg.sb(es, "stga%d" % i, [128, 1616], F32) for i in range(2)]
                win = load_weight_bf16(es, "win", w["w_in"][l], D, INC, gain_src=w["mix_norm_g"][l], chunk=1616, stg=stg_a)
                wq = load_weight_bf16(es, "wq", w["w_q_up"][l], QL, H * QK, gain_src=w["q_norm_g"][l], chunk=768, stg=stg_a)
                wkv = load_weight_bf16(es, "wkv", w["w_kv_up"][l], KVL, H * 128, gain_src=w["kv_norm_g"][l], chunk=1024, stg=stg_a)
                wqsw = pg.sb(es, "wqsw", [128, 3, H, QK], BF16)
                wkn = pg.sb(es, "wkn", [128, 2, H, QK], BF16)
                wv = pg.sb(es, "wv", [128, 2, H, VH], BF16)
                pg.op("pool", lambda e: e.memset(wqsw.t[:, :, :, :], 0.0), writes=[wqsw])
                pg.op("pool", lambda e: e.memset(wkn.t[:, :, :, :], 0.0), writes=[wkn])
                for k in range(3):
                    qv = wq.t[:, k, :].rearrange("p (h c) -> p h c", h=H)
                    pg.op("dve", lambda e, k=k, qv=qv: e.tensor_copy(wqsw.t[:, k, :, 64:80], qv[:, :, 80:96]), reads=[wq], writes=[wqsw])
                    pg.op("dve", lambda e, k=k, qv=qv: e.tensor_copy(wqsw.t[:, k, :, 80:96], qv[:, :, 64:80]), reads=[wq], writes=[wqsw])
                for k in range(2):
                    kv = wkv.t[:, k, :].rearrange("p (h c) -> p h c", h=H)
                    pg.op("dve", lambda e, k=k, kv=kv: e.tensor_copy(wkn.t[:, k, :, 0:64], kv[:, :, 0:64]), reads=[wkv], writes=[wkn])
                    pg.op("dve", lambda e, k=k, kv=kv: e.tensor_copy(wv.t[:, k, :, :], kv[:, :, 64:128]), reads=[wkv], writes=[wv])
                bg = pg.sb(es, "bg", [128, 16], F32)
                pg.dma("pool", bg.t[:, :], w["b_gate"][l].rearrange("a (i p) -> p (a i)", p=128), writes=[bg], allow_slow_non_contiguous=True)
                gcols = pg.sb(es, "gcols", [96, 4], F32)
                for ci, nm in ((0, "q_head_g"), (2, "k_head_g")):
                    gsrc = w[nm][l].rearrange("(p o) -> p o", o=1)
                    pg.dma("pool", gcols.t[0:96, ci:ci + 1], gsrc, writes=[gcols], allow_slow_non_contiguous=True)
                    pg.dma("pool", gcols.t[0:64, ci + 1:ci + 2], gsrc[0:64, :], writes=[gcols], allow_slow_non_contiguous=True)
                    pg.dma("pool", gcols.t[64:80, ci + 1:ci + 2], gsrc[80:96, :], writes=[gcols], allow_slow_non_contiguous=True)
                    pg.dma("pool", gcols.t[80:96, ci + 1:ci + 2], gsrc[64:80, :], writes=[gcols], allow_slow_non_contiguous=True)
                pg.op("dve", lambda e: e.tensor_scalar(gcols.t[0:96, 0:2], gcols.t[0:96, 0:2], float(QK ** -0.5), None, ALU.mult),
                      reads=[gcols], writes=[gcols])
                xq = [pg.sb(es, "xq%d" % i, [128, D], F32) for i in range(2)]
                xn = pg.sb(es, "xn", [128, D], BF16)
                junk = pg.sb(es, "junk", [128, D], BF16)
                ss = pg.sb(es, "ss", [128, 1], F32)
                hT = pg.sb(es, "hT", [128, 8, 512], BF16)
                ut = [pg.sb(es, "ut%d" % i, [128, 512], BF16) for i in range(2)]
                gtb = [pg.sb(es, "gtb%d" % i, [128, 512], BF16) for i in range(2)]
                cq = pg.sb(es, "cq", [128, 3, 512], F32)
                sq = pg.sb(es, "sq", [128, 3, 512], BF16)
                cqns = [pg.sb(es, "cqn%d" % i, [128, 3, 512], BF16) for i in range(2)]
                ckvns = [pg.sb(es, "ckvn%d" % i, [128, 2, 512], BF16) for i in range(2)]
                rq = pg.sb(es, "rq", [128, 512], F32)
                krTs = [pg.sb(es, "krT%d" % i, [32, 512], BF16) for i in range(2)]
                cst = [pg.sb(es, "cst%d" % i, [96, 2, 512], F32) for i in range(2)]
                sqh = [pg.sb(es, "sqh%d" % i, [96, 512], BF16) for i in range(4)]
                rh = [pg.sb(es, "rh%d" % i, [96, 512], F32) for i in range(4)]
                t1 = [pg.sb(es, "t1%d" % i, [96, 512], F32) for i in range(4)]
                t2 = [pg.sb(es, "t2%d" % i, [96, 512], F32) for i in range(4)]
                qo = [pg.sb(es, "qo%d" % i, [96, 512], BF16) for i in range(4)]
                cgs = [pg.sb(es, "cg%d" % i, [96, 4, 512], F32) for i in range(2)]
                kswb = pg.sb(es, "kswb", [96, 512], F32)
                hfc = [0]
                vt = [pg.sb(es, "vt%d" % i, [128, H, VH + 1], BF16) for i in range(2)]
                for i in range(2):
                    pg.op("pool", lambda e, i=i: e.memset(vt[i].t[:, :, :], 1.0), writes=[vt[i]])
                nps = [0]

                def nextps():
                    nps[0] += 1
                    return ps[nps[0] % 6]

                def colnorm(tiles_f32, n, width, dst_bf):
                    pb = nextps()
                    for i in range(n):
                        pg.op("pe", lambda e, i=i, pb=pb: e.matmul(pb.t[:, :], ones.t[:, :], sq.t[:, i, :], start=(i == 0), stop=(i == n - 1)),
                              reads=[ones, sq], writes=[pb])
                    pg.op("act", lambda e, pb=pb: e.activation(rq.t[:, :], pb.t[:, :], AF.Sqrt, bias=epsc.t[:, 0:1], scale=1.0 / width),
                          reads=[pb, epsc], writes=[rq])
                    pg.op("dve", lambda e: e.reciprocal(rq.t[:, :], rq.t[:, :]), reads=[rq], writes=[rq])
                    for i in range(n):
                        pg.op("dve", lambda e, i=i: e.tensor_tensor(dst_bf.t[:, i, :], tiles_f32.t[:, i, :], rq.t[:, :], ALU.mult),
                              reads=[tiles_f32, rq], writes=[dst_bf])

                def headfin(hh, pq, psw, bconst, cgi, dst_ap, cg):
                    j = hfc[0] % 4
                    hfc[0] += 1
                    pg.op("act", lambda e: e.activation(sqh[j].t[:, :], pq.t[0:96, :], AF.Square), reads=[pq], writes=[sqh[j]])
                    pn = nextps()
                    pg.op("pe", lambda e: e.matmul(pn.t[0:96, :], ones.t[0:96, 0:96], sqh[j].t[:, :], start=True, stop=True),
                          reads=[ones, sqh[j]], writes=[pn])
                    pg.op("dve", lambda e: e.tensor_tensor(t1[j].t[:, :], pq.t[0:96, :], cg.t[:, cgi, :], ALU.mult), reads=[pq, cg], writes=[t1[j]])
                    if psw is not None:
                        pg.op("dve", lambda e: e.tensor_tensor(t2[j].t[:, :], psw.t[0:96, :], cg.t[:, cgi + 1, :], ALU.mult), reads=[psw, cg], writes=[t2[j]])
                        pg.op("pool", lambda e: e.tensor_tensor(t1[j].t[:, :], t1[j].t[:, :], t2[j].t[:, :], ALU.add), reads=[t1[j], t2[j]], writes=[t1[j]])
                    else:
                        pg.op("pool", lambda e: e.tensor_tensor(t1[j].t[:, :], t1[j].t[:, :], bconst.t[:, :], ALU.add), reads=[t1[j], bconst], writes=[t1[j]])
                    pg.op("act", lambda e: e.activation(rh[j].t[:, :], pn.t[0:96, :], AF.Sqrt, bias=epsc.t[0:96, 0:1], scale=1.0 / QK),
                          reads=[pn, epsc], writes=[rh[j]])
                    pg.op("dve", lambda e: e.reciprocal(rh[j].t[:, :], rh[j].t[:, :]), reads=[rh[j]], writes=[rh[j]])
                    pg.op("dve", lambda e: e.tensor_tensor(qo[j].t[:, :], t1[j].t[:, :], rh[j].t[:, :], ALU.mult), reads=[t1[j], rh[j]], writes=[qo[j]])
                    pg.dma("sp", dst_ap, qo[j].t[:, :], reads=[qo[j]])

                def prep(tb):
                    c0, c1 = tb * 512, (tb + 1) * 512
                    cs, cg = cst[tb % 2], cgs[tb % 2]
                    pg.dma("pool", cs.t[:, :, :], cs_tab[:, :, c0:c1].rearrange("a p t -> p a t"), writes=[cs])
                    for ci in range(4):
                        pg.op("pool", lambda e, ci=ci: e.tensor_scalar(cg.t[:, ci, :], cs.t[:, ci % 2, :], gcols.t[0:96, ci:ci + 1], None, ALU.mult),
                              reads=[cs, gcols], writes=[cg])
                    for tt in range(4):
                        xb = xq[tt % 2]
                        pg.dma("sp", xb.t[:, :], src[c0 + tt * 128:c0 + (tt + 1) * 128, :], writes=[xb])
                        pg.op("act", lambda e, xb=xb: e.activation(junk.t[:, :], xb.t[:, :], AF.Square, accum_out=ss.t[:, 0:1]),
                              reads=[xb], writes=[junk, ss])
                        pg.op("act", lambda e: e.activation(ss.t[:, 0:1], ss.t[:, 0:1], AF.Sqrt, bias=epsc.t[:, 0:1], scale=1.0 / D),
                              reads=[ss, epsc], writes=[ss])
                        pg.op("dve", lambda e: e.reciprocal(ss.t[:, 0:1], ss.t[:, 0:1]), reads=[ss], writes=[ss])
                        pg.op("dve", lambda e, xb=xb: e.tensor_scalar(xn.t[:, :], xb.t[:, :], ss.t[:, 0:1], None, ALU.mult),
                              reads=[xb, ss], writes=[xn])
                        pt = psT[tt % 2]
                        for k in range(8):
                            pg.op("pe", lambda e, k=k, pt=pt: e.transpose(pt.t[:, k * 128:(k + 1) * 128], xn.t[:, k * 128:(k + 1) * 128], idb.t[:, :]),
                                  reads=[xn, idb], writes=[pt])
                        pg.op("act", lambda e, tt=tt, pt=pt: e.copy(hT.t[:, :, tt * 128:(tt + 1) * 128],
                                                                   pt.t[:, :].rearrange("p (k t) -> p k t", k=8)),
                              reads=[pt], writes=[hT])

                def proj(col0, m):
                    pb = nextps()
                    for k in range(8):
                        pg.op("pe", lambda e, k=k, pb=pb: e.matmul(pb.t[0:m, :], win.t[:, k, col0:col0 + m], hT.t[:, k, :],
                                                                 start=(k == 0), stop=(k == 7)), reads=[win, hT], writes=[pb])
                    return pb

                def j_units(tb):
                    c0, c1 = tb * 512, (tb + 1) * 512
                    cqn, ckvn, krT = cqns[tb % 2], ckvns[tb % 2], krTs[tb % 2]
                    units = []

                    def u_unit(i):
                        pb = proj(i * 128, 128)
                        u = ut[i % 2]
                        pg.op("act", lambda e: e.copy(u.t[:, :], pb.t[:, :]), reads=[pb], writes=[u])
                        pg.dma("sp", uT[i * 128:(i + 1) * 128, c0:c1], u.t[:, :], reads=[u])

                    def c_unit(col, i):
                        pb = proj(col + i * 128, 128)
                        pg.op("dve", lambda e: e.tensor_copy(cq.t[:, i, :], pb.t[:, :]), reads=[pb], writes=[cq])
                        pg.op("act", lambda e: e.activation(sq.t[:, i, :], pb.t[:, :], AF.Square), reads=[pb], writes=[sq])

                    def kr_unit():
                        pb = proj(1152, 32)
                        pg.op("act", lambda e: e.copy(krT.t[:, :], pb.t[0:32, :]), reads=[pb], writes=[krT])

                    def g_unit(i):
                        pb = proj(1184 + i * 128, 128)
                        gb = gtb[i % 2]
                        pg.op("act", lambda e: e.activation(gb.t[:, :], pb.t[:, :], AF.Sigmoid, bias=bg.t[:, i:i + 1]),
                              reads=[pb, bg], writes=[gb])
                        pg.dma("pool", gT[i * 128:(i + 1) * 128, c0:c1], gb.t[:, :], reads=[gb])
                    for i in range(3):
                        units.append(lambda i=i: c_unit(512, i))
                    units.append(lambda: colnorm(cq, 3, QL, cqn))
                    for i in range(2):
                        units.append(lambda i=i: c_unit(896, i))
                    units.append(lambda: colnorm(cq, 2, KVL, ckvn))
                    units.append(kr_unit)
                    for i in range(4):
                        units.append(lambda i=i: u_unit(i))
                    for i in range(16):
                        units.append(lambda i=i: g_unit(i))
                    return units

                def h_units(tb):
                    c0, c1 = tb * 512, (tb + 1) * 512
                    cqn, ckvn, krT, cg = cqns[tb % 2], ckvns[tb % 2], krTs[tb % 2], cgs[tb % 2]
                    units = []

                    def q_unit(hh):
                        pq = nextps()
                        for k in range(3):
                            pg.op("pe", lambda e, k=k: e.matmul(pq.t[0:96, :], wq.t[:, k, hh * QK:(hh + 1) * QK], cqn.t[:, k, :],
                                                              start=(k == 0), stop=(k == 2)), reads=[wq, cqn], writes=[pq])
                        psw = nextps()
                        for k in range(3):
                            pg.op("pe", lambda e, k=k: e.matmul(psw.t[0:96, :], wqsw.t[:, k, hh, :], cqn.t[:, k, :],
                                                              start=(k == 0), stop=(k == 2)), reads=[wqsw, cqn], writes=[psw])
                        headfin(hh, pq, psw, None, 0, qT[hh, :, c0:c1], cg)

                    def ksw_unit():
                        psw = nextps()
                        pg.op("pe", lambda e: e.matmul(psw.t[0:96, :], einj.t[:, 1, :], krT.t[:, :], start=True, stop=True),
                              reads=[einj, krT], writes=[psw])
                        pg.op("dve", lambda e: e.tensor_tensor(kswb.t[:, :], psw.t[0:96, :], cg.t[:, 3, :], ALU.mult), reads=[psw, cg], writes=[kswb])

                    def k_unit(hh):
                        pk = nextps()
                        for k in range(2):
                            pg.op("pe", lambda e, k=k: e.matmul(pk.t[0:96, :], wkn.t[:, k, hh, :], ckvn.t[:, k, :],
                                                              start=(k == 0), stop=False), reads=[wkn, ckvn], writes=[pk])
                        pg.op("pe", lambda e: e.matmul(pk.t[0:96, :], einj.t[:, 0, :], krT.t[:, :], start=False, stop=True),
                              reads=[einj, krT], writes=[pk])
                        headfin(hh, pk, None, kswb, 2, kT[hh, :, c0:c1], cg)

                    def v_unit(tt):
                        pv = nextps()
                        for k in range(2):
                            pg.op("pe", lambda e, k=k: e.matmul(pv.t[:, :], ckvn.t[:, k, tt * 128:(tt + 1) * 128],
                                                              wv.t[:, k, :, :].rearrange("p h c -> p (h c)"),
                                                              start=(k == 0), stop=(k == 1)), reads=[ckvn, wv], writes=[pv])
                        v = vt[tt % 2]
                        pg.op("act", lambda e: e.copy(v.t[:, :, 0:VH], pv.t[:, :].rearrange("p (h c) -> p h c", h=H)),
                              reads=[pv], writes=[v])
                        pg.dma("sp", vA[c0 + tt * 128:c0 + (tt + 1) * 128, :, :], v.t[:, :, :], reads=[v])
                    for hh in range(H):
                        units.append(lambda hh=hh: q_unit(hh))
                    units.append(ksw_unit)
                    for hh in range(H):
                        units.append(lambda hh=hh: k_unit(hh))
                    for tt in range(4):
                        units.append(lambda tt=tt: v_unit(tt))
                    return units

                def interleave(a, b):
                    ia = ib = 0
                    while ia < len(a) or ib < len(b):
                        if ia < len(a):
                            a[ia]()
                            ia += 1
                        if ib < len(b):
                            b[ib]()
                            ib += 1

                NBA = NB if KSTOP > 0 else 0
                pend = []
                if NBA > 0:
                    prep(0)
                for tb in range(NBA):
                    interleave(j_units(tb), pend)
                    if tb + 1 < NBA:
                        prep(tb + 1)
                    pend = h_units(tb)
                interleave([], pend)
            pg.barrier()


        def phase_c(l):
            with ExitStack() as es:
                pw2 = [Buf(es.enter_context(nc.psum_tensor("pw%d_%d" % (l, i), [128, 1024], F32)), True) for i in range(3)]
                psc = [Buf(es.enter_context(nc.psum_tensor("pc%d_%d" % (l, i), [128, 512], F32)), True) for i in range(2)]
                vall = pg.sb(es, "vall", [128, NT, H, VH + 1], BF16)
                pg.dma("sp", vall.t[:, :, :, :], vA.rearrange("(t p) h c -> p t h c", p=128), writes=[vall])
                kth = [pg.sb(es, "kth%d" % i, [96, T], BF16) for i in range(2)]
                qtb = [pg.sb(es, "qtb%d" % i, [96, 512], BF16) for i in range(2)]
                pT = [pg.sb(es, "pT%d" % i, [128, 1024], BF16) for i in range(3)]
                scp = [pg.sb(es, "scp%d" % i, [128, 1024], F32) for i in range(2)]
                osb = [pg.sb(es, "osb%d" % i, [VH + 1, 512], F32) for i in range(2)]
                rec = pg.sb(es, "rec", [VH, 512], F32)
                oo = [pg.sb(es, "oo%d" % i, [VH, 512], BF16) for i in range(2)]
                selt = pg.sb(es, "selt", [VH + 1, VH], F32)
                pg.dma("pool", selt.t[:, :], sel_d, writes=[selt])
                it = 0
                for hh in range(H):
                    kt_ = kth[hh % 2]
                    pg.dma("pool", kt_.t[:, :], kT[hh, :, :], writes=[kt_])
                    for qb in range(NB):
                        q_ = qtb[it % 2]
                        pg.dma("sp", q_.t[:, :], qT[hh, :, qb * 512:(qb + 1) * 512], writes=[q_])
                        po = psc[it % 2]
                        pd = psc[it % 2]

                        NP = NT // 2

                        def qk(kp):
                            pb = pw2[kp % 3]
                            for hf_ in range(2):
                                kt = kp * 2 + hf_
                                pg.op("pe", lambda e, kt=kt, hf_=hf_: e.matmul(pb.t[:, hf_ * 512:(hf_ + 1) * 512], kt_.t[:, kt * 128:(kt + 1) * 128], q_.t[:, :],
                                                                            start=True, stop=True), reads=[kt_, q_], writes=[pb])
                        qk(0)
                        if NP > 1:
                            qk(1)
                        for kp in range(NP):
                            pb = pw2[kp % 3]
                            p_ = pT[kp % 3]
                            if kp % 2 == 0 or not CSPLIT:
                                pg.op("act", lambda e, pb=pb, p_=p_: e.activation(p_.t[:, :], pb.t[:, :], AF.Exp), reads=[pb], writes=[p_])
                            else:
                                sc_ = scp[(kp // 2) % 2]
                                pg.op("dve", lambda e, pb=pb, sc_=sc_: e.tensor_copy(sc_.t[:, :], pb.t[:, :]), reads=[pb], writes=[sc_])
                                pg.op("act", lambda e, sc_=sc_, p_=p_: e.activation(p_.t[:, :], sc_.t[:, :], AF.Exp), reads=[sc_], writes=[p_])
                            if kp + 2 < NP:
                                qk(kp + 2)
                            for hf_ in range(2):
                                kt = kp * 2 + hf_
                                pg.op("pe", lambda e, kt=kt, hf_=hf_, p_=p_: e.matmul(po.t[0:VH + 1, :], vall.t[:, kt, hh, :], p_.t[:, hf_ * 512:(hf_ + 1) * 512],
                                                                             start=(kt == 0), stop=(kt == NT - 1)), reads=[vall, p_], writes=[po])
                        o_ = osb[it % 2]
                        pg.op("dve", lambda e, o_=o_: e.tensor_copy(o_.t[:, :], po.t[0:VH + 1, :]), reads=[po], writes=[o_])
                        pg.op("pe", lambda e, o_=o_: e.matmul(pd.t[0:VH, :], selt.t[:, :], o_.t[:, :], start=True, stop=True),
                              reads=[selt, o_], writes=[pd])
                        pg.op("dve", lambda e: e.reciprocal(rec.t[:, :], pd.t[0:VH, :]), reads=[pd], writes=[rec])
                        ob = oo[it % 2]
                        pg.op("dve", lambda e, o_=o_, ob=ob: e.tensor_tensor(ob.t[:, :], o_.t[0:VH, :], rec.t[:, :], ALU.mult),
                              reads=[o_, rec], writes=[ob])
                        pg.dma("sp", ymT[hh * VH:(hh + 1) * VH, qb * 512:(qb + 1) * 512], ob.t[:, :], reads=[ob])
                        it += 1
            pg.barrier()

        def phase_d(l, src, dst):
            with ExitStack() as es:
                psd = [Buf(es.enter_context(nc.psum_tensor("pd%d_%d" % (l, i), [128, 512], F32)), True) for i in range(8)]
                wos = load_weight_bf16(es, "wos", w["w_out_ssm"][l], SW, D, chunk=1024)
                wom = load_weight_bf16(es, "wom", w["w_out_mla"][l], SW, D, chunk=1024)
                wo = load_weight_bf16(es, "wo", w["w_o"][l], D, D, chunk=1024)
                ysb = [pg.sb(es, "ysb%d" % i, [128, 4, 512], BF16) for i in range(2)]
                ymb = [pg.sb(es, "ymb%d" % i, [128, 4, 512], BF16) for i in range(2)]
                gb = [pg.sb(es, "gb%d" % i, [128, 2, 512], BF16) for i in range(2)]
                m0 = [pg.sb(es, "m0%d" % i, [128, 512], F32) for i in range(2)]
                m1 = [pg.sb(es, "m1%d" % i, [128, 512], F32) for i in range(2)]
                mgs = [pg.sb(es, "mg%d" % i, [128, 8, 512], BF16) for i in range(2)]
                xt = [pg.sb(es, "xd%d" % i, [128, 4, D], F32) for i in range(2)]
                cnt = [0]

                def merge(tb):
                    c0, c1 = tb * 512, (tb + 1) * 512
                    ys_, ym_, xb, mg = ysb[tb % 2], ymb[tb % 2], xt[tb % 2], mgs[tb % 2]
                    pg.dma("sp", ys_.t[:, :, :], ysT[:, c0:c1].rearrange("(k p) t -> p k t", p=128), writes=[ys_])
                    pg.dma("pool", ym_.t[:, :, :], ymT[:, c0:c1].rearrange("(k p) t -> p k t", p=128), writes=[ym_])
                    pg.dma("sp", xb.t[:, :, :], src[c0:c1, :].rearrange("(t p) d -> p t d", p=128), writes=[xb])
                    for ft in range(8):
                        n = cnt[0]
                        cnt[0] += 1
                        g_ = gb[n % 2]
                        pg.dma("pool", g_.t[:, :, :], gT[:, c0:c1].rearrange("(a f) t -> f a t", a=2)[ft * 128:(ft + 1) * 128, :, :], writes=[g_])
                        p0, p1 = psd[(2 * n) % 4], psd[(2 * n + 1) % 4]
                        for k in range(4):
                            pg.op("pe", lambda e, k=k, p0=p0: e.matmul(p0.t[:, :], wos.t[:, k, ft * 128:(ft + 1) * 128], ys_.t[:, k, :],
                                                                     start=(k == 0), stop=(k == 3)), reads=[wos, ys_], writes=[p0])
                        for k in range(4):
                            pg.op("pe", lambda e, k=k, p1=p1: e.matmul(p1.t[:, :], wom.t[:, k, ft * 128:(ft + 1) * 128], ym_.t[:, k, :],
                                                                     start=(k == 0), stop=(k == 3)), reads=[wom, ym_], writes=[p1])
                        a0, a1 = m0[n % 2], m1[n % 2]
                        pg.op("dve", lambda e, a0=a0, p0=p0, g_=g_: e.tensor_tensor(a0.t[:, :], p0.t[:, :], g_.t[:, 0, :], ALU.mult), reads=[p0, g_], writes=[a0])
                        pg.op("dve", lambda e, a1=a1, p1=p1, g_=g_: e.tensor_tensor(a1.t[:, :], p1.t[:, :], g_.t[:, 1, :], ALU.mult), reads=[p1, g_], writes=[a1])
                        pg.op("pool", lambda e, a0=a0, a1=a1, mg=mg, ft=ft: e.tensor_tensor(mg.t[:, ft, :], a0.t[:, :], a1.t[:, :], ALU.add), reads=[a0, a1], writes=[mg])

                def outproj(tb):
                    c0, c1 = tb * 512, (tb + 1) * 512
                    xb, mg = xt[tb % 2], mgs[tb % 2]
                    for tt in range(4):
                        for ch in range(2):
                            pb = psd[4 + (tt * 2 + ch) % 4]
                            for k in range(8):
                                pg.op("pe", lambda e, k=k, pb=pb: e.matmul(pb.t[:, :], mg.t[:, k, tt * 128:(tt + 1) * 128], wo.t[:, k, ch * 512:(ch + 1) * 512],
                                                                         start=(k == 0), stop=(k == 7)), reads=[mg, wo], writes=[pb])
                            pg.op("dve", lambda e, pb=pb: e.tensor_tensor(xb.t[:, tt, ch * 512:(ch + 1) * 512], pb.t[:, :],
                                                                         xb.t[:, tt, ch * 512:(ch + 1) * 512], ALU.add), reads=[pb, xb], writes=[xb])
                    pg.dma("sp", dst[c0:c1, :].rearrange("(t p) d -> p t d", p=128), xb.t[:, :, :], reads=[xb])

                merge(0)
                for tb in range(NB):
                    if tb + 1 < NB:
                        merge(tb + 1)
                    outproj(tb)
            pg.barrier()


        def phase_b(l):
            NCH = T // 8
            CH = min(512, NCH)
            NH = NCH // CH
            LV = int(round(math.log2(NCH)))
            assert (1 << LV) == NCH
            TWO_PI = 2.0 * math.pi

            def TT(eng, ob, oap, ab, aap, bb, bap, op):
                pg.op(eng, lambda e: e.tensor_tensor(oap, aap, bap, op), reads=[ab, bb], writes=[ob])

            def TS(eng, ob, oap, ab, aap, s1, s2, op0, op1=None):
                if op1 is None:
                    pg.op(eng, lambda e: e.tensor_scalar(oap, aap, s1, None, op0), reads=[ab], writes=[ob])
                else:
                    pg.op(eng, lambda e: e.tensor_scalar(oap, aap, s1, s2, op0, op1), reads=[ab], writes=[ob])

            with ExitStack() as es:
                psb = [Buf(es.enter_context(nc.psum_tensor("pb%d_%d" % (l, i), [128, 512], F32)), True) for i in range(8)]

                def sm(name, shape=(128, 2, 16), dt=F32):
                    return pg.sb(es, name, list(shape), dt)
                lre, lim, lsb, st, lr, th, mag = [sm(n) for n in ("lre", "lim", "lsb", "st", "lr", "th", "mag")]
                tq, kf, phi, s1, c1, s2, c2, s4, c4, tmpa, tmpb = [sm(n) for n in ("tq", "kf", "phi", "s1", "c1", "s2", "c2", "s4", "c4", "tmpa", "tmpb")]
                ki = sm("ki", dt=I32)
                ar, ai, nr, den, fr, fi = [sm(n) for n in ("ar", "ai", "nr", "den", "fr", "fi")]
                hp = sm("hp", (128, 1))
                pg.op("pool", lambda e: e.memset(hp.t[:, :], math.pi / 2.0), writes=[hp])
                A3 = lambda b: b.t[:, :, :]
                pg.dma("sp", A3(lre), w["ssm_lam_re"][l].rearrange("d (gp two) n -> (two n) d gp", two=2), writes=[lre], allow_slow_non_contiguous=True)
                pg.dma("pool", A3(lim), w["ssm_lam_im"][l].rearrange("d (gp two) n -> (two n) d gp", two=2), writes=[lim], allow_slow_non_contiguous=True)
                lsv = w["ssm_log_step"][l].rearrange("d (gp two) -> two d gp", two=2)
                for two in range(2):
                    pg.dma("sp", lsb.t[two * 64:(two + 1) * 64, :, :], lsv[two].partition_broadcast(64), writes=[lsb], allow_slow_non_contiguous=True)
                pg.op("act", lambda e: e.activation(A3(st), A3(lsb), AF.Exp), reads=[lsb], writes=[st])
                TT("dve", lr, A3(lr), lre, A3(lre), st, A3(st), ALU.mult)
                TT("dve", th, A3(th), lim, A3(lim), st, A3(st), ALU.mult)
                pg.op("act", lambda e: e.activation(A3(mag), A3(lr), AF.Exp), reads=[lr], writes=[mag])
                TS("dve", tq, A3(tq), th, A3(th), 1.0 / TWO_PI, None, ALU.mult)
                pg.op("dve", lambda e: e.tensor_copy(A3(ki), A3(tq)), reads=[tq], writes=[ki])
                pg.op("dve", lambda e: e.tensor_copy(A3(kf), A3(ki)), reads=[ki], writes=[kf])
                pg.op("dve", lambda e: e.scalar_tensor_tensor(A3(phi), A3(kf), -TWO_PI, A3(th), ALU.mult, ALU.add), reads=[kf, th], writes=[phi])
                pg.op("act", lambda e: e.activation(A3(s1), A3(phi), AF.Sin, scale=0.25), reads=[phi], writes=[s1])
                pg.op("act", lambda e: e.activation(A3(c1), A3(phi), AF.Sin, bias=hp.t[:, 0:1], scale=0.25), reads=[phi, hp], writes=[c1])
                TT("dve", tmpa, A3(tmpa), s1, A3(s1), c1, A3(c1), ALU.mult)
                TS("dve", s2, A3(s2), tmpa, A3(tmpa), 2.0, None, ALU.mult)
                TT("dve", tmpb, A3(tmpb), s1, A3(s1), s1, A3(s1), ALU.mult)
                TS("dve", c2, A3(c2), tmpb, A3(tmpb), -2.0, 1.0, ALU.mult, ALU.add)
                TT("dve", tmpa, A3(tmpa), s2, A3(s2), c2, A3(c2), ALU.mult)
                TS("dve", s4, A3(s4), tmpa, A3(tmpa), 2.0, None, ALU.mult)
                TT("dve", tmpb, A3(tmpb), s2, A3(s2), s2, A3(s2), ALU.mult)
                TS("dve", c4, A3(c4), tmpb, A3(tmpb), -2.0, 1.0, ALU.mult, ALU.add)
                TT("dve", ar, A3(ar), mag, A3(mag), c4, A3(c4), ALU.mult)
                TT("dve", ai, A3(ai), mag, A3(mag), s4, A3(s4), ALU.mult)
                TS("dve", nr, A3(nr), ar, A3(ar), -1.0, None, ALU.add)
                TT("dve", tmpa, A3(tmpa), lre, A3(lre), lre, A3(lre), ALU.mult)
                TT("dve", tmpb, A3(tmpb), lim, A3(lim), lim, A3(lim), ALU.mult)
                TT("dve", den, A3(den), tmpa, A3(tmpa), tmpb, A3(tmpb), ALU.add)
                pg.op("dve", lambda e: e.reciprocal(A3(den), A3(den)), reads=[den], writes=[den])
                TT("dve", tmpa, A3(tmpa), nr, A3(nr), lre, A3(lre), ALU.mult)
                TT("dve", tmpb, A3(tmpb), ai, A3(ai), lim, A3(lim), ALU.mult)
                TT("dve", tmpa, A3(tmpa), tmpa, A3(tmpa), tmpb, A3(tmpb), ALU.add)
                TT("dve", fr, A3(fr), tmpa, A3(tmpa), den, A3(den), ALU.mult)
                TT("dve", tmpa, A3(tmpa), ai, A3(ai), lre, A3(lre), ALU.mult)
                TT("dve", tmpb, A3(tmpb), nr, A3(nr), lim, A3(lim), ALU.mult)
                TT("dve", tmpa, A3(tmpa), tmpa, A3(tmpa), tmpb, A3(tmpb), ALU.subtract)
                TT("dve", fi, A3(fi), tmpa, A3(tmpa), den, A3(den), ALU.mult)
                PWr = sm("PWr", (128, 9, 2, 16))
                PWi = sm("PWi", (128, 9, 2, 16))
                ALr = sm("ALr", (128, LV, 2, 16))
                ALi = sm("ALi", (128, LV, 2, 16))
                nALi = sm("nALi", (128, LV, 2, 16))
                pg.op("pool", lambda e: e.memset(PWr.t[:, 0, :, :], 1.0), writes=[PWr])
                pg.op("pool", lambda e: e.memset(PWi.t[:, 0, :, :], 0.0), writes=[PWi])

                def cmul(orb, orap, oib, oiap, xrb, xrap, xib, xiap, yrb, yrap, yib, yiap):
                    TT("dve", tmpa, A3(tmpa), xrb, xrap, yrb, yrap, ALU.mult)
                    TT("dve", tmpb, A3(tmpb), xib, xiap, yib, yiap, ALU.mult)
                    TT("dve", orb, orap, tmpa, A3(tmpa), tmpb, A3(tmpb), ALU.subtract)
                    TT("dve", tmpa, A3(tmpa), xrb, xrap, yib, yiap, ALU.mult)
                    TT("dve", tmpb, A3(tmpb), xib, xiap, yrb, yrap, ALU.mult)
                    TT("dve", oib, oiap, tmpa, A3(tmpa), tmpb, A3(tmpb), ALU.add)
                for k in range(1, 9):
                    cmul(PWr, PWr.t[:, k, :, :], PWi, PWi.t[:, k, :, :], PWr, PWr.t[:, k - 1, :, :], PWi, PWi.t[:, k - 1, :, :],
                         ar, A3(ar), ai, A3(ai))
                pg.op("pool", lambda e: e.tensor_copy(ALr.t[:, 0, :, :], PWr.t[:, 8, :, :]), reads=[PWr], writes=[ALr])
                pg.op("pool", lambda e: e.tensor_copy(ALi.t[:, 0, :, :], PWi.t[:, 8, :, :]), reads=[PWi], writes=[ALi])
                for m in range(1, LV):
                    cmul(ALr, ALr.t[:, m, :, :], ALi, ALi.t[:, m, :, :], ALr, ALr.t[:, m - 1, :, :], ALi, ALi.t[:, m - 1, :, :],
                         ALr, ALr.t[:, m - 1, :, :], ALi, ALi.t[:, m - 1, :, :])
                TS("dve", nALi, nALi.t[:, :, :, :], ALi, ALi.t[:, :, :, :], -1.0, None, ALU.mult)
                bbr = sm("bbr", (128, 2, 16, 32))
                bbi = sm("bbi", (128, 2, 16, 32))
                ctr = sm("ctr", (128, 16, 32))
                cti = sm("cti", (128, 16, 32))
                nctr = sm("nctr", (128, 16, 32))
                ncti = sm("ncti", (128, 16, 32))
                dcol = sm("dcol", (128, 4))
                bmask = sm("bmask", (128, 128))
                pg.dma("sp", bmask.t[:, :], bmask_d, writes=[bmask])
                pg.dma("pool", dcol.t[:, :], w["ssm_d"][l].rearrange("(gt a) p -> (a p) gt", gt=4), writes=[dcol], allow_slow_non_contiguous=True)
                with ExitStack() as es2:
                    bre = pg.sb(es2, "bre", [128, 2, 16, 32], F32)
                    bim = pg.sb(es2, "bim", [128, 2, 16, 32], F32)
                    wt1 = pg.sb(es2, "wt1", [128, 2, 16, 32], F32)
                    wt2 = pg.sb(es2, "wt2", [128, 2, 16, 32], F32)
                    A4 = lambda b: b.t[:, :, :, :]
                    pg.op("pool", lambda e: e.memset(A4(bre), 0.0), writes=[bre])
                    pg.op("pool", lambda e: e.memset(A4(bim), 0.0), writes=[bim])
                    pg.op("pool", lambda e: e.memset(ctr.t[:, :, :], 0.0), writes=[ctr])
                    pg.op("pool", lambda e: e.memset(cti.t[:, :, :], 0.0), writes=[cti])
                    n = 0
                    for two in range(2):
                        for (dst_, nm) in ((bre, "ssm_b_re"), (bim, "ssm_b_im")):
                            srcv = w[nm][l].rearrange("d (gp two) n q -> two n d gp q", two=2)[two]
                            for d_ in range(2):
                                pg.dma("sp" if n % 2 == 0 else "pool", dst_.t[two * 64:(two + 1) * 64, d_, :, two * 16:(two + 1) * 16],
                                       srcv[:, d_, :, :], writes=[dst_], allow_slow_non_contiguous=True)
                                n += 1
                        for (dst_, nm) in ((ctr, "ssm_c_re"), (cti, "ssm_c_im")):
                            srcv = w[nm][l].rearrange("(gp two) p n -> two n gp p", two=2)[two]
                            for g4 in range(16):
                                pg.dma("sp" if n % 2 == 0 else "pool", dst_.t[two * 64:(two + 1) * 64, g4, two * 16:(two + 1) * 16],
                                       srcv[:, g4, :], writes=[dst_], allow_slow_non_contiguous=True)
                                n += 1
                    TS("dve", nctr, nctr.t[:, :, :], ctr, ctr.t[:, :, :], -1.0, None, ALU.mult)
                    TS("dve", ncti, ncti.t[:, :, :], cti, cti.t[:, :, :], -1.0, None, ALU.mult)
                    frb = fr.t[:, :, :].unsqueeze(3).to_broadcast([128, 2, 16, 32])
                    fib = fi.t[:, :, :].unsqueeze(3).to_broadcast([128, 2, 16, 32])
                    TT("dve", wt1, A4(wt1), bre, A4(bre), fr, frb, ALU.mult)
                    TT("dve", wt2, A4(wt2), bim, A4(bim), fi, fib, ALU.mult)
                    TT("dve", bbr, A4(bbr), wt1, A4(wt1), wt2, A4(wt2), ALU.subtract)
                    TT("dve", wt1, A4(wt1), bim, A4(bim), fr, frb, ALU.mult)
                    TT("dve", wt2, A4(wt2), bre, A4(bre), fi, fib, ALU.mult)
                    TT("dve", bbi, A4(bbi), wt1, A4(wt1), wt2, A4(wt2), ALU.add)
                    pg.barrier()
                nc.leave_named_scope() if False else None
                TEr = sm("TEr", (128, 8, 2, 4, 32))
                TEi = sm("TEi", (128, 8, 2, 4, 32))
                wa = sm("wa", (128, 2, 4, 32))
                wb = sm("wb", (128, 2, 4, 32))
                WS = sm("WS", (128, 2, 2, 8, 128), BF16)
                WY = sm("WY", (128, 2, 2, 8, 4, 32), BF16)
                ya = sm("ya", (128, 4, 32))
                yb = sm("yb", (128, 4, 32))
                Km = sm("Km", (128, 15, 128), BF16)
                ktmp = sm("ktmp", (128, 128))
                usb = sm("usb", (128, T), BF16)
                usd = sm("usd", (128, 8, NCH), BF16)
                X16 = sm("X16", (128, 2, 4, 2, NCH + 1), BF16)
                Xm_t = sm("Xm", (128, 4, 2, NCH))
                XA = [Buf(Xm_t.t) for _ in range(4)]
                ysb_ = sm("ysbf", (128, CH * 8))
                g1 = [sm("g1%d" % i, (128, 1024)) for i in range(2)]
                g2 = [sm("g2%d" % i, (128, 1024)) for i in range(2)]
                ygb = [sm("ygb%d" % i, (128, 1024), BF16) for i in range(2)]
                for gt in range(4):
                    gs = slice(gt * 4, gt * 4 + 4)
                    pg.dma("sp", usb.t[:, :], uT[gt * 128:(gt + 1) * 128, :], writes=[usb])
                    hN = NCH // 2
                    pg.op("act", lambda e: e.copy(usd.t[:, :, 0:hN], usb.t[:, 0:hN * 8].rearrange("p (c j) -> p j c", j=8)), reads=[usb], writes=[usd])
                    pg.op("pool", lambda e: e.tensor_copy(usd.t[:, :, hN:NCH], usb.t[:, hN * 8:NCH * 8].rearrange("p (c j) -> p j c", j=8)), reads=[usb], writes=[usd])
                    A4w = lambda b: b.t[:, :, :, :]
                    for e_ in range(8):
                        prb = PWr.t[:, e_, :, gs].unsqueeze(3).to_broadcast([128, 2, 4, 32])
                        pib = PWi.t[:, e_, :, gs].unsqueeze(3).to_broadcast([128, 2, 4, 32])
                        TT("dve", wa, A4w(wa), bbr, bbr.t[:, :, gs, :], PWr, prb, ALU.mult)
                        TT("dve", wb, A4w(wb), bbi, bbi.t[:, :, gs, :], PWi, pib, ALU.mult)
                        TT("dve", TEr, TEr.t[:, e_, :, :, :], wa, A4w(wa), wb, A4w(wb), ALU.subtract)
                        TT("dve", wa, A4w(wa), bbi, bbi.t[:, :, gs, :], PWr, prb, ALU.mult)
                        TT("dve", wb, A4w(wb), bbr, bbr.t[:, :, gs, :], PWi, pib, ALU.mult)
                        TT("dve", TEi, TEi.t[:, e_, :, :, :], wa, A4w(wa), wb, A4w(wb), ALU.add)
                    n = 0
                    for ri, TE in ((0, TEr), (1, TEi)):
                        for d_ in range(2):
                            for j0 in (0, 4):
                                pb = psb[n % 2]
                                for jj in range(4):
                                    j = j0 + jj
                                    e_ = 7 - j if d_ == 0 else j
                                    pg.op("pe", lambda e, pb=pb, jj=jj, e_=e_, d_=d_, TE=TE: e.transpose(
                                        pb.t[:, jj * 128:(jj + 1) * 128], TE.t[:, e_, d_, :, :].rearrange("p g c -> p (g c)"), idf.t[:, :]),
                                        reads=[TE, idf], writes=[pb])
                                pg.op("act", lambda e, pb=pb, ri=ri, d_=d_, j0=j0: e.copy(
                                    WS.t[:, ri, d_, j0:j0 + 4, :].rearrange("p j c -> p (j c)"), pb.t[:, :]), reads=[pb], writes=[WS])
                                n += 1
                    for d_ in range(2):
                        for i in range(8):
                            pw = i + 1 if d_ == 0 else 8 - i
                            prb = PWr.t[:, pw, d_, gs].unsqueeze(2).to_broadcast([128, 4, 32])
                            pib = PWi.t[:, pw, d_, gs].unsqueeze(2).to_broadcast([128, 4, 32])
                            A3y = lambda b: b.t[:, :, :]
                            TT("dve", ya, A3y(ya), ctr, ctr.t[:, gs, :], PWr, prb, ALU.mult)
                            TT("dve", yb, A3y(yb), ncti, ncti.t[:, gs, :], PWi, pib, ALU.mult)
                            TT("dve", WY, WY.t[:, d_, 0, i, :, :], ya, A3y(ya), yb, A3y(yb), ALU.add)
                            TT("dve", ya, A3y(ya), nctr, nctr.t[:, gs, :], PWi, pib, ALU.mult)
                            TT("dve", yb, A3y(yb), ncti, ncti.t[:, gs, :], PWr, prb, ALU.mult)
                            TT("dve", WY, WY.t[:, d_, 1, i, :, :], ya, A3y(ya), yb, A3y(yb), ALU.add)
                    ctr_v = ctr.t[:, gs, :].rearrange("p g c -> p (g c)")
                    ncti_v = ncti.t[:, gs, :].rearrange("p g c -> p (g c)")
                    for ti in range(15):
                        tau = ti - 7
                        lst = [(0, tau)] if tau > 0 else ([(1, -tau)] if tau < 0 else [(0, 0), (1, 0)])
                        pk = psb[6 + ti % 2]
                        nm_ = len(lst) * 2
                        q_ = 0
                        for (d_, e_) in lst:
                            for (TE, cv, cb) in ((TEr, ctr_v, ctr), (TEi, ncti_v, ncti)):
                                pg.op("pe", lambda e, TE=TE, cv=cv, d_=d_, e_=e_, q_=q_, pk=pk: e.matmul(
                                    pk.t[:, 0:128], TE.t[:, e_, d_, :, :].rearrange("p g c -> p (g c)"), cv,
                                    start=(q_ == 0), stop=(q_ == nm_ - 1)), reads=[TE, cb], writes=[pk])
                                q_ += 1
                        if tau != 0:
                            pg.op("dve", lambda e, pk=pk, ti=ti: e.tensor_tensor(Km.t[:, ti, :], pk.t[:, 0:128], bmask.t[:, :], ALU.mult),
                                  reads=[pk, bmask], writes=[Km])
                        else:
                            pg.op("dve", lambda e, pk=pk: e.tensor_tensor(ktmp.t[:, :], pk.t[:, 0:128], bmask.t[:, :], ALU.mult),
                                  reads=[pk, bmask], writes=[ktmp])
                            pg.op("dve", lambda e, ti=ti: e.scalar_tensor_tensor(Km.t[:, ti, :], idf.t[:, :], dcol.t[:, gt:gt + 1], ktmp.t[:, :],
                                                                                ALU.mult, ALU.add), reads=[idf, dcol, ktmp], writes=[Km])
                    pg.op("pool", lambda e: e.memset(X16.t[:, 0, :, :, 0:1], 0.0), writes=[X16])
                    pg.op("pool", lambda e: e.memset(X16.t[:, 1, :, :, NCH:NCH + 1], 0.0), writes=[X16])
                    if True:
                        n = 0
                        for d_ in range(2):
                            for ri in range(2):
                                for hf in range(NH):
                                    pbs = [psb[2 + pp] for pp in range(4)]
                                    for j in range(8):
                                        for pp in range(4):
                                            pg.op("pe", lambda e, pb=pbs[pp], pp=pp, ri=ri, d_=d_, j=j, hf=hf: e.matmul(
                                                pb.t[:, 0:CH], WS.t[32 * pp:32 * pp + 32, ri, d_, j, :],
                                                usd.t[32 * pp:32 * pp + 32, j, hf * CH:(hf + 1) * CH],
                                                start=(j == 0), stop=(j == 7), tile_position=(32 * pp, 0)),
                                                reads=[WS, usd], writes=[pbs[pp]])
                                    for pp in range(4):
                                        pg.op("act" if pp % 2 == 0 else "dve", lambda e, pb=pbs[pp], pp=pp, ri=ri, hf=hf: (
                                            e.copy if hasattr(e, "copy") else e.tensor_copy)(
                                            XA[pp].t[:, pp, ri, hf * CH:(hf + 1) * CH], pb.t[:, 0:CH]), reads=[pbs[pp]], writes=[XA[pp]])
                            for pp in range(4):
                                gp = gt * 4 + pp
                                X_ = XA[pp]
                                sched = [(m, (1 << (m + 1)) - 1, NCH >> (m + 1)) for m in range(LV)]
                                sched += [(m, (1 << (m + 1)) + (1 << m) - 1, (NCH >> (m + 1)) - 1) for m in range(LV - 2, -1, -1)]
                                for (m, p0, cnt) in sched:
                                    if cnt <= 0:
                                        continue
                                    strd = 1 << (m + 1)
                                    sft = 1 << m
                                    a_r = ALr.t[:, m, d_, gp:gp + 1]
                                    a_i = ALi.t[:, m, d_, gp:gp + 1]
                                    na_i = nALi.t[:, m, d_, gp:gp + 1]
                                    if d_ == 0:
                                        d0 = p0
                                        s0 = p0 - sft
                                    else:
                                        d0 = NCH - 1 - (p0 + (cnt - 1) * strd)
                                        s0 = d0 + sft
                                    dsl = slice(d0, d0 + (cnt - 1) * strd + 1, strd)
                                    ssl = slice(s0, s0 + (cnt - 1) * strd + 1, strd)
                                    for (ori, sri, sc) in ((0, 0, a_r), (0, 1, na_i), (1, 0, a_i), (1, 1, a_r)):
                                        pg.op("dve" if d_ == 0 else SCAN_ENG1, lambda e, ori=ori, sri=sri, sc=sc, dsl=dsl, ssl=ssl, pp=pp, X_=X_: e.scalar_tensor_tensor(
                                            X_.t[:, pp, ori, dsl], X_.t[:, pp, sri, ssl], sc, X_.t[:, pp, ori, dsl], ALU.mult, ALU.add),
                                            reads=[X_, ALr, ALi, nALi], writes=[X_])
                                off = 1 if d_ == 0 else 0
                                pg.op("act", lambda e, X_=X_, pp=pp, d_=d_, off=off: e.copy(X16.t[:, d_, pp, :, off:off + NCH], X_.t[:, pp, :, :]),
                                      reads=[X_], writes=[X16])
                    if True:
                        n = 0
                        for hf in range(NH):
                            for i in range(8):
                                py = psb[6 + n % 2]
                                n += 1
                                for j in range(8):
                                    st0 = hf * CH * 8 + j
                                    pg.op("pe", lambda e, py=py, i=i, j=j, hf=hf: e.matmul(
                                        py.t[:, 0:CH], Km.t[:, (i - j) + 7, :], usd.t[:, j, hf * CH:(hf + 1) * CH],
                                        start=(j == 0), stop=False), reads=[Km, usd], writes=[py])
                                q_ = 0
                                for d_ in range(2):
                                    for ri in range(2):
                                        for pp in range(4):
                                            off = hf * CH + (0 if d_ == 0 else 1)
                                            pg.op("pe", lambda e, py=py, pp=pp, d_=d_, ri=ri, off=off, i=i, q_=q_: e.matmul(
                                                py.t[32 * pp:32 * pp + 32, 0:CH], WY.t[:, d_, ri, i, pp, :], X16.t[:, d_, pp, ri, off:off + CH],
                                                start=False, stop=(q_ == 15), tile_position=(0, 32 * pp)), reads=[WY, X16], writes=[py])
                                            q_ += 1
                                pg.op("act", lambda e, py=py, i=i: e.copy(ysb_.t[:, i:i + (CH - 1) * 8 + 1:8], py.t[:, 0:CH]), reads=[py], writes=[ysb_])
                            for cc in range(0, CH * 8, 1024):
                                cw = min(1024, CH * 8 - cc)
                                k_ = (cc // 1024) % 2
                                yv = ysb_.t[:, cc:cc + cw]
                                a_, b_, o_ = g1[k_], g2[k_], ygb[k_]
                                pg.op("pool", lambda e, a_=a_, yv=yv, cw=cw: e.tensor_tensor(a_.t[:, 0:cw], yv, yv, ALU.mult), reads=[ysb_], writes=[a_])
                                pg.op("pool", lambda e, a_=a_, cw=cw: e.tensor_scalar(a_.t[:, 0:cw], a_.t[:, 0:cw], 0.044715, 1.0, ALU.mult, ALU.add),
                                      reads=[a_], writes=[a_])
                                pg.op("pool", lambda e, a_=a_, yv=yv, cw=cw: e.tensor_tensor(a_.t[:, 0:cw], a_.t[:, 0:cw], yv, ALU.mult), reads=[a_, ysb_], writes=[a_])
                                pg.op("act", lambda e, a_=a_, b_=b_, cw=cw: e.activation(b_.t[:, 0:cw], a_.t[:, 0:cw], AF.Sigmoid, scale=1.5957691216057308),
                                      reads=[a_], writes=[b_])
                                pg.op("dve", lambda e, b_=b_, o_=o_, yv=yv, cw=cw: e.tensor_tensor(o_.t[:, 0:cw], b_.t[:, 0:cw], yv, ALU.mult),
                                      reads=[b_, ysb_], writes=[o_])
                                t0_ = hf * CH * 8 + cc
                                pg.dma("sp", ygT[gt * 128:(gt + 1) * 128, t0_:t0_ + cw], o_.t[:, 0:cw], reads=[o_])
            pg.barrier()
            with ExitStack() as es:
                psg = [Buf(es.enter_context(nc.psum_tensor("pg%d_%d" % (l, i), [128, 512], F32)), True) for i in range(4)]
                wglu = load_weight_bf16(es, "wglu", w["w_glu"][l], SW, SW, chunk=512)
                bgl = pg.sb(es, "bgl", [128, 4], F32)
                pg.dma("pool", bgl.t[:, :], w["b_glu"][l].rearrange("(k p) -> p k", p=128), writes=[bgl], allow_slow_non_contiguous=True)
                ygi = [pg.sb(es, "ygi%d" % i, [128, 4, 512], BF16) for i in range(2)]
                sgb = [pg.sb(es, "sgb%d" % i, [128, 512], F32) for i in range(2)]
                yo = [pg.sb(es, "yo%d" % i, [128, 512], BF16) for i in range(2)]
                n = 0
                for tb in range(NB):
                    c0, c1 = tb * 512, (tb + 1) * 512
                    yg_ = ygi[tb % 2]
                    pg.dma("sp", yg_.t[:, :, :], ygT[:, c0:c1].rearrange("(k p) t -> p k t", p=128), writes=[yg_])
                    for ft in range(4):
                        pb = psg[n % 4]
                        for k in range(4):
                            pg.op("pe", lambda e, k=k, pb=pb, ft=ft: e.matmul(pb.t[:, :], wglu.t[:, k, ft * 128:(ft + 1) * 128], yg_.t[:, k, :],
                                                                            start=(k == 0), stop=(k == 3)), reads=[wglu, yg_], writes=[pb])
                        sg_, y_ = sgb[n % 2], yo[n % 2]
                        pg.op("act", lambda e, pb=pb, sg_=sg_, ft=ft: e.activation(sg_.t[:, :], pb.t[:, :], AF.Sigmoid, bias=bgl.t[:, ft:ft + 1]),
                              reads=[pb, bgl], writes=[sg_])
                        pg.op("dve", lambda e, sg_=sg_, y_=y_, ft=ft: e.tensor_tensor(y_.t[:, :], sg_.t[:, :], yg_.t[:, ft, :], ALU.mult),
                              reads=[sg_, yg_], writes=[y_])
                        pg.dma("pool", ysT[ft * 128:(ft + 1) * 128, c0:c1], y_.t[:, :], reads=[y_])
                        n += 1
            pg.barrier()

        src = x_in
        for l in range(nlayers):
            dst = y_out if l == nlayers - 1 else xs
            if "a" in phases:
                with nc.named_scope("A%d" % l):
                    phase_a(l, src)
            if "b" in phases:
                with nc.named_scope("B%d" % l):
                    phase_b(l)
            if "c" in phases:
                with nc.named_scope("C%d" % l):
                    phase_c(l)
            if "d" in phases:
                with nc.named_scope("D%d" % l):
                    phase_d(l, src, x1)
            if "f" in phases:
                with nc.named_scope("F%d" % l):
                    phase_ffn(l, x1 if "d" in phases else src, dst)
            src = dst
        pg.barrier()
    return nc


_CACHE = {}
RUN_KW = {}


def _consts(T):
    half = ROPE // 2
    inv = (10000.0 ** (-np.arange(half, dtype=np.float32) / half)).astype(np.float32)
    ang = np.arange(T, dtype=np.float32)[:, None] * inv[None, :]
    cos = np.cos(ang).astype(np.float32).T
    sin = np.sin(ang).astype(np.float32).T
    cs = np.zeros((2, QK, T), np.float32)
    cs[0, :NOPE] = 1.0
    cs[0, NOPE:NOPE + half] = cos
    cs[0, NOPE + half:] = cos
    cs[1, NOPE:NOPE + half] = -sin
    cs[1, NOPE + half:] = sin
    e = np.zeros((32, 2, QK), np.float32)
    for i in range(32):
        e[i, 0, NOPE + i] = 1.0
        e[i, 1, NOPE + (i + 16) % 32] = 1.0
    return {
        "cs_tab": cs,
        "bmask": np.kron(np.eye(4, dtype=np.float32), np.ones((32, 32), np.float32)),
        "sel": np.concatenate([np.zeros((VH, VH), np.float32), np.ones((1, VH), np.float32)], axis=0),
        "ones_b": np.ones((128, 128), np.float32).astype(ml_dtypes.bfloat16),
        "einj": e.astype(ml_dtypes.bfloat16),
        "ident_f": np.eye(128, dtype=np.float32),
        "ident_b": np.eye(128, dtype=np.float32).astype(ml_dtypes.bfloat16),
    }


def run(inputs, T, nlayers, ncores, debug=False, phases="abcdf"):
    key = (T, nlayers, debug, phases)
    if key not in _CACHE:
        _CACHE[key] = build(T, nlayers, debug, phases)
    nc = _CACHE[key]
    base = {k: np.ascontiguousarray(np.asarray(v, dtype=np.float32)) for k, v in inputs.items() if k != "x"}
    base.update(_consts(T))
    x = np.asarray(inputs["x"], dtype=np.float32)
    in_maps = []
    for c in range(ncores):
        m = dict(base)
        m["x"] = np.ascontiguousarray(x[c % x.shape[0], :T, :])
        in_maps.append(m)
    c0 = int(os.environ.get('KCORE', '0'))
    res = run_bass_kernel_spmd(nc, in_maps, core_ids=list(range(c0, c0 + ncores)), **RUN_KW)
    return res


def kernel(**inputs):
    x = np.asarray(inputs["x"])
    B, T, _ = x.shape
    res = run(inputs, T, DEPTH, 8)
    out = np.stack([np.asarray(res.results[b]["y"], dtype=np.float32) for b in range(B)], axis=0)
    return out.astype(np.float32)
```

```python
import math
import os
KSTOP = int(os.environ.get('KSTOP', '99'))
CSPLIT = int(os.environ.get('CSPLIT', '0'))
from contextlib import ExitStack
import numpy as np
import ml_dtypes
import concourse.bass as bass
import concourse.mybir as mybir
from concourse.bass_utils import run_bass_kernel_spmd

F32 = mybir.dt.float32
BF16 = mybir.dt.bfloat16
I32 = mybir.dt.int32
AF = mybir.ActivationFunctionType
ALU = mybir.AluOpType
AX = mybir.AxisListType

D = 1024
DEPTH = 4
SW = 512
G = 32
NS = 64
P16 = 16
H = 8
QK = 96
NOPE = 64
ROPE = 32
VH = 64
QL = 384
KVL = 256
DFF = 4096
INC = 3232
EPS = 1e-6
NSLOT = 12


class Buf:
    __slots__ = ("t", "lw", "rd", "psum")

    def __init__(self, t, psum=False):
        self.t = t
        self.psum = psum
        self.lw = None
        self.rd = {}


class Prog:
    def __init__(self, nc):
        self.nc = nc
        self.eng = {"pe": nc.tensor, "act": nc.scalar, "dve": nc.vector, "pool": nc.gpsimd, "sp": nc.sync}
        self.sem = {}
        self.cnt = {}
        for k in ["pe", "act", "dve", "pool"]:
            self.sem[k] = nc.alloc_semaphore("s_" + k)
            self.cnt[k] = 0
        for i in range(NSLOT):
            for pre in ("d", "g"):
                k = "%s%d" % (pre, i)
                self.sem[k] = nc.alloc_semaphore("s_" + k)
                self.cnt[k] = 0
        self.ndma_q = {"d": 0, "g": 0}
        self.seen = {e: {} for e in self.eng}
        self.ndma = 0
        self.stack = None

    def _wait(self, e, tok):
        k, v = tok
        if self.seen[e].get(k, 0) >= v:
            return
        self.seen[e][k] = v
        self.eng[e].wait_ge(self.sem[k], v)

    def _deps(self, e, reads, writes):
        for b in reads:
            if b.lw is not None:
                self._wait(e, b.lw)
            if b.psum:
                for k, v in b.rd.items():
                    if k != e:
                        self._wait(e, (k, v))
        for b in writes:
            if b.lw is not None and not (b.lw[0] == e and e == "pe"):
                self._wait(e, b.lw)
            for k, v in b.rd.items():
                if k == e and e == "pe":
                    continue
                self._wait(e, (k, v))

    def _mark(self, tok, reads, writes):
        for b in reads:
            b.rd[tok[0]] = max(b.rd.get(tok[0], 0), tok[1])
        for b in writes:
            b.lw = tok
            b.rd = {}

    def op(self, e, fn, reads=(), writes=()):
        self._deps(e, reads, writes)
        ins = fn(self.eng[e])
        self.cnt[e] += 1
        ins.then_inc(self.sem[e], 1)
        tok = (e, self.cnt[e])
        self.seen[e][e] = max(self.seen[e].get(e, 0), 0)
        self._mark(tok, reads, writes)
        return tok

    def dma(self, e, out, in_, reads=(), writes=(), **kw):
        self._deps(e, reads, writes)
        pre = "g" if e == "pool" else "d"
        k = "%s%d" % (pre, self.ndma_q[pre] % NSLOT)
        self.ndma_q[pre] += 1
        if self.cnt[k] > 0:
            self._wait(e, (k, self.cnt[k]))
        ins = self.eng[e].dma_start(out=out, in_=in_, **kw)
        self.cnt[k] += 16
        ins.then_inc(self.sem[k], 16)
        tok = (k, self.cnt[k])
        self._mark(tok, reads, writes)
        return tok

    def barrier(self):
        for e in self.eng:
            for k in self.sem:
                if k != e and self.cnt[k] > 0:
                    self._wait(e, (k, self.cnt[k]))

    def sb(self, es, name, shape, dt):
        self.nname = getattr(self, "nname", 0) + 1
        return Buf(es.enter_context(self.nc.sbuf_tensor("%s_%d" % (name, self.nname), list(shape), dt)))


def bcast_row(ap_1d, parts):
    return ap_1d.unsqueeze(0).partition_broadcast(parts) if hasattr(ap_1d, "partition_broadcast") else ap_1d


def build(T, nlayers, debug=False, phases="abcdf"):
    nc = bass.Bass("TRN2", target_bir_lowering=False)
    NB = T // 512
    NT = T // 128

    def din(name, shape, dt=F32):
        return nc.dram_tensor(name, list(shape), dt, kind="ExternalInput").ap()

    def dscr(name, shape, dt):
        kind = "ExternalOutput" if debug else "Internal"
        return nc.dram_tensor(name, list(shape), dt, kind=kind).ap()

    x_in = din("x", [T, D])
    w = {}
    shapes = {
        "mix_norm_g": [DEPTH, D], "w_in": [DEPTH, D, INC], "b_gate": [DEPTH, 2, D],
        "ssm_lam_re": [DEPTH, 2, G, NS], "ssm_lam_im": [DEPTH, 2, G, NS], "ssm_log_step": [DEPTH, 2, G],
        "ssm_b_re": [DEPTH, 2, G, NS, P16], "ssm_b_im": [DEPTH, 2, G, NS, P16],
        "ssm_c_re": [DEPTH, G, P16, NS], "ssm_c_im": [DEPTH, G, P16, NS], "ssm_d": [DEPTH, G, P16],
        "w_glu": [DEPTH, SW, SW], "b_glu": [DEPTH, SW], "w_out_ssm": [DEPTH, SW, D],
        "q_norm_g": [DEPTH, QL], "kv_norm_g": [DEPTH, KVL], "w_q_up": [DEPTH, QL, H * QK],
        "w_kv_up": [DEPTH, KVL, H * 128], "q_head_g": [DEPTH, QK], "k_head_g": [DEPTH, QK],
        "w_out_mla": [DEPTH, SW, D], "w_o": [DEPTH, D, D], "ffn_norm_g": [DEPTH, D],
        "w_ff1": [DEPTH, D, DFF], "w_ff2": [DEPTH, DFF, D],
    }
    for k, s in shapes.items():
        w[k] = din(k, s)
    ident_f = din("ident_f", [128, 128])
    ident_b = din("ident_b", [128, 128], BF16)
    y_out = nc.dram_tensor("y", [T, D], F32, kind="ExternalOutput").ap()
    xs = dscr("xs", [T, D], F32)
    x1 = dscr("x1", [T, D], F32)
    uT = dscr("uT", [SW, T], BF16)
    qT = dscr("qT", [H, QK, T], BF16)
    kT = dscr("kT", [H, QK, T], BF16)
    vA = dscr("vA", [T, H, VH + 1], BF16)
    gT = dscr("gT", [2 * D, T], BF16)
    ysT = dscr("ysT", [SW, T], BF16)
    ymT = dscr("ymT", [SW, T], BF16)
    cs_tab = din("cs_tab", [2, QK, T])
    ones_d = din("ones_b", [128, 128], BF16)
    einj_d = din("einj", [32, 2, QK], BF16)
    sel_d = din("sel", [VH + 1, VH])
    bmask_d = din("bmask", [128, 128])
    ygT = dscr("ygT", [SW, T], BF16)

    pg = Prog(nc)
    ps = []
    psT = []
    pctr = [0]

    def alloc_ps(es):
        pctr[0] += 1
        ps[:] = [Buf(es.enter_context(nc.psum_tensor("psb%d_%d" % (pctr[0], i), [128, 512], F32)), True) for i in range(6)]
        psT[:] = [Buf(es.enter_context(nc.psum_tensor("pst%d_%d" % (pctr[0], i), [128, 1024], BF16)), True) for i in range(2)]

    with ExitStack() as g_es:
        idb = pg.sb(g_es, "idb", [128, 128], BF16)
        idf = pg.sb(g_es, "idf", [128, 128], F32)
        epsc = pg.sb(g_es, "epsc", [128, 1], F32)
        pg.dma("sp", idb.t[:, :], ident_b, writes=[idb])
        pg.dma("sp", idf.t[:, :], ident_f, writes=[idf])
        pg.op("dve", lambda e: e.memset(epsc.t[:, :], EPS), writes=[epsc])
        ones = pg.sb(g_es, "ones", [128, 128], BF16)
        einj = pg.sb(g_es, "einj", [32, 2, QK], BF16)
        pg.dma("sp", ones.t[:, :], ones_d, writes=[ones])
        pg.dma("sp", einj.t[:, :, :], einj_d, writes=[einj])

        def load_weight_bf16(es, name, src, K, N, gain_src=None, chunk=2048, stg=None):
            kt = K // 128
            wt = pg.sb(es, name, [128, kt, N], BF16)
            gcol = None
            if gain_src is not None:
                gcol = pg.sb(es, name + "_g", [128, kt], F32)
                pg.dma("pool", gcol.t[:, :], gain_src.rearrange("(k p) -> p k", p=128), writes=[gcol],
                       allow_slow_non_contiguous=True)
            if stg is None:
                stg = [pg.sb(es, name + "_s%d" % i, [128, chunk], F32) for i in range(2)]
            n = 0
            for k in range(kt):
                for c0 in range(0, N, chunk):
                    cw = min(chunk, N - c0)
                    s = stg[n % 2]
                    pg.dma("sp" if n % 2 == 0 else "pool", s.t[:, 0:cw], src[k * 128:(k + 1) * 128, c0:c0 + cw],
                           writes=[s])
                    eng = "act" if n % 2 == 0 else "dve"
                    if gcol is None:
                        if eng == "act":
                            pg.op("act", lambda e, s=s, k=k, c0=c0, cw=cw: e.copy(wt.t[:, k, c0:c0 + cw], s.t[:, 0:cw]),
                                  reads=[s], writes=[wt])
                        else:
                            pg.op("dve", lambda e, s=s, k=k, c0=c0, cw=cw: e.tensor_copy(wt.t[:, k, c0:c0 + cw], s.t[:, 0:cw]),
                                  reads=[s], writes=[wt])
                    else:
                        pg.op("dve", lambda e, s=s, k=k, c0=c0, cw=cw: e.tensor_scalar(
                            wt.t[:, k, c0:c0 + cw], s.t[:, 0:cw], gcol.t[:, k:k + 1], None, ALU.mult),
                            reads=[s, gcol], writes=[wt])
                    n += 1
            return wt

        def rms_rows(es_bufs, xt, width, out_bf):
            junk, ss = es_bufs
            pg.op("act", lambda e: e.activation(junk.t[:, 0:width], xt.t[:, 0:width], AF.Square, accum_out=ss.t[:, 0:1]),
                  reads=[xt], writes=[junk, ss])
            pg.op("act", lambda e: e.activation(ss.t[:, 0:1], ss.t[:, 0:1], AF.Sqrt, bias=epsc.t[:, 0:1], scale=1.0 / width),
                  reads=[ss, epsc], writes=[ss])
            pg.op("dve", lambda e: e.reciprocal(ss.t[:, 0:1], ss.t[:, 0:1]), reads=[ss], writes=[ss])
            pg.op("dve", lambda e: e.tensor_scalar(out_bf.t[:, 0:width], xt.t[:, 0:width], ss.t[:, 0:1], None, ALU.mult),
                  reads=[xt, ss], writes=[out_bf])

        def phase_ffn(l, src, dst):
            for half in range(2):
                with ExitStack() as es:
                    alloc_ps(es)
                    hs = DFF // 2
                    nk = hs // 128
                    stg_f = [pg.sb(es, "stgf%d" % i, [128, 2048], F32) for i in range(2)]
                    w1 = load_weight_bf16(es, "w1", w["w_ff1"][l][:, half * hs:(half + 1) * hs], D, hs,
                                          gain_src=w["ffn_norm_g"][l], stg=stg_f)
                    w2 = load_weight_bf16(es, "w2", w["w_ff2"][l][half * hs:(half + 1) * hs, :], hs, D, chunk=1024, stg=stg_f)
                    xt = [pg.sb(es, "xt%d" % i, [128, 4, D], F32) for i in range(2)]
                    xa = [pg.sb(es, "xa%d" % i, [128, 4, D], F32) for i in range(2)] if half == 1 else xt
                    xn = pg.sb(es, "xn", [128, D], BF16)
                    junk = pg.sb(es, "junk", [128, D], BF16)
                    ss = pg.sb(es, "ss", [128, 1], F32)
                    hTs = [pg.sb(es, "hT%d" % i, [128, 8, 512], BF16) for i in range(2)]
                    hid = pg.sb(es, "hid", [128, nk, 512], BF16)
                    rl = [pg.sb(es, "rl%d" % i, [128, 512], F32) for i in range(2)]

                    def prep(tb):
                        xb, hT = xt[tb % 2], hTs[tb % 2]
                        pg.dma("sp", xb.t[:, :, :], src[tb * 512:(tb + 1) * 512, :].rearrange("(t p) d -> p t d", p=128), writes=[xb])
                        if half == 1:
                            pg.dma("pool", xa[tb % 2].t[:, :, :], dst[tb * 512:(tb + 1) * 512, :].rearrange("(t p) d -> p t d", p=128),
                                   writes=[xa[tb % 2]])
                        for tt in range(4):
                            pg.op("act", lambda e, tt=tt: e.activation(junk.t[:, :], xb.t[:, tt, :], AF.Square, accum_out=ss.t[:, 0:1]),
                                  reads=[xb], writes=[junk, ss])
                            pg.op("act", lambda e: e.activation(ss.t[:, 0:1], ss.t[:, 0:1], AF.Sqrt, bias=epsc.t[:, 0:1], scale=1.0 / D),
                                  reads=[ss, epsc], writes=[ss])
                            pg.op("dve", lambda e: e.reciprocal(ss.t[:, 0:1], ss.t[:, 0:1]), reads=[ss], writes=[ss])
                            pg.op("dve", lambda e, tt=tt: e.tensor_scalar(xn.t[:, :], xb.t[:, tt, :], ss.t[:, 0:1], None, ALU.mult),
                                  reads=[xb, ss], writes=[xn])
                            pt = psT[tt % 2]
                            for k in range(8):
                                pg.op("pe", lambda e, k=k, pt=pt: e.transpose(pt.t[:, k * 128:(k + 1) * 128], xn.t[:, k * 128:(k + 1) * 128], idb.t[:, :]),
                                      reads=[xn, idb], writes=[pt])
                            pg.op("dve", lambda e, tt=tt, pt=pt: e.tensor_copy(hT.t[:, :, tt * 128:(tt + 1) * 128],
                                                                              pt.t[:, :].rearrange("p (k t) -> p k t", k=8)),
                                  reads=[pt], writes=[hT])

                    prep(0)
                    for tb in range(NB):
                        hT, xo = hTs[tb % 2], xa[tb % 2]
                        for ft in range(nk):
                            pb = ps[ft % 4]
                            for k in range(8):
                                pg.op("pe", lambda e, k=k, ft=ft, pb=pb: e.matmul(pb.t[:, :], w1.t[:, k, ft * 128:(ft + 1) * 128], hT.t[:, k, :],
                                                                               start=(k == 0), stop=(k == 7)),
                                      reads=[w1, hT], writes=[pb])
                            r = rl[ft % 2]
                            pg.op("act", lambda e, pb=pb, r=r: e.activation(r.t[:, :], pb.t[:, :], AF.Relu), reads=[pb], writes=[r])
                            pg.op("pool", lambda e, r=r, ft=ft: e.tensor_tensor(hid.t[:, ft, :], r.t[:, :], r.t[:, :], ALU.mult),
                                  reads=[r], writes=[hid])
                        if tb + 1 < NB:
                            prep(tb + 1)
                        for tt in range(4):
                            for ch in range(2):
                                pb = ps[4 + ch]
                                for k in range(nk):
                                    pg.op("pe", lambda e, k=k, tt=tt, ch=ch, pb=pb: e.matmul(
                                        pb.t[:, :], hid.t[:, k, tt * 128:(tt + 1) * 128], w2.t[:, k, ch * 512:(ch + 1) * 512],
                                        start=(k == 0), stop=(k == nk - 1)), reads=[hid, w2], writes=[pb])
                                pg.op("dve", lambda e, tt=tt, ch=ch, pb=pb: e.tensor_tensor(
                                    xo.t[:, tt, ch * 512:(ch + 1) * 512], pb.t[:, :], xo.t[:, tt, ch * 512:(ch + 1) * 512], ALU.add),
                                    reads=[pb, xo], writes=[xo])
                        pg.dma("sp", dst[tb * 512:(tb + 1) * 512, :].rearrange("(t p) d -> p t d", p=128), xo.t[:, :, :], reads=[xo])
                pg.barrier()

        def phase_a(l, src):
            with ExitStack() as es:
                alloc_ps(es)
                stg_a = [pg.sb(es, "stga%d" % i, [128, 1616], F32) for i in range(2)]
                win = load_weight_bf16(es, "win", w["w_in"][l], D, INC, gain_src=w["mix_norm_g"][l], chunk=1616, stg=stg_a)
                wq = load_weight_bf16(es, "wq", w["w_q_up"][l], QL, H * QK, gain_src=w["q_norm_g"][l], chunk=768, stg=stg_a)
                wkv = load_weight_bf16(es, "wkv", w["w_kv_up"][l], KVL, H * 128, gain_src=w["kv_norm_g"][l], chunk=1024, stg=stg_a)
                wqsw = pg.sb(es, "wqsw", [128, 3, H, QK], BF16)
                wkn = pg.sb(es, "wkn", [128, 2, H, QK], BF16)
                wv = pg.sb(es, "wv", [128, 2, H, VH], BF16)
                pg.op("pool", lambda e: e.memset(wqsw.t[:, :, :, :], 0.0), writes=[wqsw])
                pg.op("pool", lambda e: e.memset(wkn.t[:, :, :, :], 0.0), writes=[wkn])
                for k in range(3):
                    qv = wq.t[:, k, :].rearrange("p (h c) -> p h c", h=H)
                    pg.op("dve", lambda e, k=k, qv=qv: e.tensor_copy(wqsw.t[:, k, :, 64:80], qv[:, :, 80:96]), reads=[wq], writes=[wqsw])
                    pg.op("dve", lambda e, k=k, qv=qv: e.tensor_copy(wqsw.t[:, k, :, 80:96], qv[:, :, 64:80]), reads=[wq], writes=[wqsw])
                for k in range(2):
                    kv = wkv.t[:, k, :].rearrange("p (h c) -> p h c", h=H)
                    pg.op("dve", lambda e, k=k, kv=kv: e.tensor_copy(wkn.t[:, k, :, 0:64], kv[:, :, 0:64]), reads=[wkv], writes=[wkn])
                    pg.op("dve", lambda e, k=k, kv=kv: e.tensor_copy(wv.t[:, k, :, :], kv[:, :, 64:128]), reads=[wkv], writes=[wv])
                bg = pg.sb(es, "bg", [128, 16], F32)
                pg.dma("pool", bg.t[:, :], w["b_gate"][l].rearrange("a (i p) -> p (a i)", p=128), writes=[bg], allow_slow_non_contiguous=True)
                gcols = pg.sb(es, "gcols", [96, 4], F32)
                for ci, nm in ((0, "q_head_g"), (2, "k_head_g")):
                    gsrc = w[nm][l].rearrange("(p o) -> p o", o=1)
                    pg.dma("pool", gcols.t[0:96, ci:ci + 1], gsrc, writes=[gcols], allow_slow_non_contiguous=True)
                    pg.dma("pool", gcols.t[0:64, ci + 1:ci + 2], gsrc[0:64, :], writes=[gcols], allow_slow_non_contiguous=True)
                    pg.dma("pool", gcols.t[64:80, ci + 1:ci + 2], gsrc[80:96, :], writes=[gcols], allow_slow_non_contiguous=True)
                    pg.dma("pool", gcols.t[80:96, ci + 1:ci + 2], gsrc[64:80, :], writes=[gcols], allow_slow_non_contiguous=True)
                pg.op("dve", lambda e: e.tensor_scalar(gcols.t[0:96, 0:2], gcols.t[0:96, 0:2], float(QK ** -0.5), None, ALU.mult),
                      reads=[gcols], writes=[gcols])
                xq = [pg.sb(es, "xq%d" % i, [128, D], F32) for i in range(2)]
                xn = pg.sb(es, "xn", [128, D], BF16)
                junk = pg.sb(es, "junk", [128, D], BF16)
                ss = pg.sb(es, "ss", [128, 1], F32)
                hT = pg.sb(es, "hT", [128, 8, 512], BF16)
                ut = [pg.sb(es, "ut%d" % i, [128, 512], BF16) for i in range(2)]
                gtb = [pg.sb(es, "gtb%d" % i, [128, 512], BF16) for i in range(2)]
                cq = pg.sb(es, "cq", [128, 3, 512], F32)
                sq = pg.sb(es, "sq", [128, 3, 512], BF16)
                cqns = [pg.sb(es, "cqn%d" % i, [128, 3, 512], BF16) for i in range(2)]
                ckvns = [pg.sb(es, "ckvn%d" % i, [128, 2, 512], BF16) for i in range(2)]
                rq = pg.sb(es, "rq", [128, 512], F32)
                krTs = [pg.sb(es, "krT%d" % i, [32, 512], BF16) for i in range(2)]
                cst = [pg.sb(es, "cst%d" % i, [96, 2, 512], F32) for i in range(2)]
                sqh = [pg.sb(es, "sqh%d" % i, [96, 512], BF16) for i in range(4)]
                rh = [pg.sb(es, "rh%d" % i, [96, 512], F32) for i in range(4)]
                t1 = [pg.sb(es, "t1%d" % i, [96, 512], F32) for i in range(4)]
                t2 = [pg.sb(es, "t2%d" % i, [96, 512], F32) for i in range(4)]
                qo = [pg.sb(es, "qo%d" % i, [96, 512], BF16) for i in range(4)]
                cgs = [pg.sb(es, "cg%d" % i, [96, 4, 512], F32) for i in range(2)]
                kswb = pg.sb(es, "kswb", [96, 512], F32)
                hfc = [0]
                vt = [pg.sb(es, "vt%d" % i, [128, H, VH + 1], BF16) for i in range(2)]
                for i in range(2):
                    pg.op("pool", lambda e, i=i: e.memset(vt[i].t[:, :, :], 1.0), writes=[vt[i]])
                nps = [0]

                def nextps():
                    nps[0] += 1
                    return ps[nps[0] % 6]

                def colnorm(tiles_f32, n, width, dst_bf):
                    pb = nextps()
                    for i in range(n):
                        pg.op("pe", lambda e, i=i, pb=pb: e.matmul(pb.t[:, :], ones.t[:, :], sq.t[:, i, :], start=(i == 0), stop=(i == n - 1)),
                              reads=[ones, sq], writes=[pb])
                    pg.op("act", lambda e, pb=pb: e.activation(rq.t[:, :], pb.t[:, :], AF.Sqrt, bias=epsc.t[:, 0:1], scale=1.0 / width),
                          reads=[pb, epsc], writes=[rq])
                    pg.op("dve", lambda e: e.reciprocal(rq.t[:, :], rq.t[:, :]), reads=[rq], writes=[rq])
                    for i in range(n):
                        pg.op("dve", lambda e, i=i: e.tensor_tensor(dst_bf.t[:, i, :], tiles_f32.t[:, i, :], rq.t[:, :], ALU.mult),
                              reads=[tiles_f32, rq], writes=[dst_bf])

                def headfin(hh, pq, psw, bconst, cgi, dst_ap, cg):
                    j = hfc[0] % 4
                    hfc[0] += 1
                    pg.op("act", lambda e: e.activation(sqh[j].t[:, :], pq.t[0:96, :], AF.Square), reads=[pq], writes=[sqh[j]])
                    pn = nextps()
                    pg.op("pe", lambda e: e.matmul(pn.t[0:96, :], ones.t[0:96, 0:96], sqh[j].t[:, :], start=True, stop=True),
                          reads=[ones, sqh[j]], writes=[pn])
                    pg.op("dve", lambda e: e.tensor_tensor(t1[j].t[:, :], pq.t[0:96, :], cg.t[:, cgi, :], ALU.mult), reads=[pq, cg], writes=[t1[j]])
                    if psw is not None:
                        pg.op("dve", lambda e: e.tensor_tensor(t2[j].t[:, :], psw.t[0:96, :], cg.t[:, cgi + 1, :], ALU.mult), reads=[psw, cg], writes=[t2[j]])
                        pg.op("pool", lambda e: e.tensor_tensor(t1[j].t[:, :], t1[j].t[:, :], t2[j].t[:, :], ALU.add), reads=[t1[j], t2[j]], writes=[t1[j]])
                    else:
                        pg.op("pool", lambda e: e.tensor_tensor(t1[j].t[:, :], t1[j].t[:, :], bconst.t[:, :], ALU.add), reads=[t1[j], bconst], writes=[t1[j]])
                    pg.op("act", lambda e: e.activation(rh[j].t[:, :], pn.t[0:96, :], AF.Sqrt, bias=epsc.t[0:96, 0:1], scale=1.0 / QK),
                          reads=[pn, epsc], writes=[rh[j]])
                    pg.op("dve", lambda e: e.reciprocal(rh[j].t[:, :], rh[j].t[:, :]), reads=[rh[j]], writes=[rh[j]])
                    pg.op("dve", lambda e: e.tensor_tensor(qo[j].t[:, :], t1[j].t[:, :], rh[j].t[:, :], ALU.mult), reads=[t1[j], rh[j]], writes=[qo[j]])
                    pg.dma("sp", dst_ap, qo[j].t[:, :], reads=[qo[j]])

                def prep(tb):
                    c0, c1 = tb * 512, (tb + 1) * 512
                    cs, cg = cst[tb % 2], cgs[tb % 2]
                    pg.dma("pool", cs.t[:, :, :], cs_tab[:, :, c0:c1].rearrange("a p t -> p a t"), writes=[cs])
                    for ci in range(4):
                        pg.op("pool", lambda e, ci=ci: e.tensor_scalar(cg.t[:, ci, :], cs.t[:, ci % 2, :], gcols.t[0:96, ci:ci + 1], None, ALU.mult),
                              reads=[cs, gcols], writes=[cg])
                    for tt in range(4):
                        xb = xq[tt % 2]
                        pg.dma("sp", xb.t[:, :], src[c0 + tt * 128:c0 + (tt + 1) * 128, :], writes=[xb])
                        pg.op("act", lambda e, xb=xb: e.activation(junk.t[:, :], xb.t[:, :], AF.Square, accum_out=ss.t[:, 0:1]),
                              reads=[xb], writes=[junk, ss])
                        pg.op("act", lambda e: e.activation(ss.t[:, 0:1], ss.t[:, 0:1], AF.Sqrt, bias=epsc.t[:, 0:1], scale=1.0 / D),
                              reads=[ss, epsc], writes=[ss])
                        pg.op("dve", lambda e: e.reciprocal(ss.t[:, 0:1], ss.t[:, 0:1]), reads=[ss], writes=[ss])
                        pg.op("dve", lambda e, xb=xb: e.tensor_scalar(xn.t[:, :], xb.t[:, :], ss.t[:, 0:1], None, ALU.mult),
                              reads=[xb, ss], writes=[xn])
                        pt = psT[tt % 2]
                        for k in range(8):
                            pg.op("pe", lambda e, k=k, pt=pt: e.transpose(pt.t[:, k * 128:(k + 1) * 128], xn.t[:, k * 128:(k + 1) * 128], idb.t[:, :]),
                                  reads=[xn, idb], writes=[pt])
                        pg.op("act", lambda e, tt=tt, pt=pt: e.copy(hT.t[:, :, tt * 128:(tt + 1) * 128],
                                                                   pt.t[:, :].rearrange("p (k t) -> p k t", k=8)),
                              reads=[pt], writes=[hT])

                def proj(col0, m):
                    pb = nextps()
                    for k in range(8):
                        pg.op("pe", lambda e, k=k, pb=pb: e.matmul(pb.t[0:m, :], win.t[:, k, col0:col0 + m], hT.t[:, k, :],
                                                                 start=(k == 0), stop=(k == 7)), reads=[win, hT], writes=[pb])
                    return pb

                def j_units(tb):
                    c0, c1 = tb * 512, (tb + 1) * 512
                    cqn, ckvn, krT = cqns[tb % 2], ckvns[tb % 2], krTs[tb % 2]
                    units = []

                    def u_unit(i):
                        pb = proj(i * 128, 128)
                        u = ut[i % 2]
                        pg.op("act", lambda e: e.copy(u.t[:, :], pb.t[:, :]), reads=[pb], writes=[u])
                        pg.dma("sp", uT[i * 128:(i + 1) * 128, c0:c1], u.t[:, :], reads=[u])

                    def c_unit(col, i):
                        pb = proj(col + i * 128, 128)
                        pg.op("dve", lambda e: e.tensor_copy(cq.t[:, i, :], pb.t[:, :]), reads=[pb], writes=[cq])
                        pg.op("act", lambda e: e.activation(sq.t[:, i, :], pb.t[:, :], AF.Square), reads=[pb], writes=[sq])

                    def kr_unit():
                        pb = proj(1152, 32)
                        pg.op("act", lambda e: e.copy(krT.t[:, :], pb.t[0:32, :]), reads=[pb], writes=[krT])

                    def g_unit(i):
                        pb = proj(1184 + i * 128, 128)
                        gb = gtb[i % 2]
                        pg.op("act", lambda e: e.activation(gb.t[:, :], pb.t[:, :], AF.Sigmoid, bias=bg.t[:, i:i + 1]),
                              reads=[pb, bg], writes=[gb])
                        pg.dma("pool", gT[i * 128:(i + 1) * 128, c0:c1], gb.t[:, :], reads=[gb])
                    for i in range(3):
                        units.append(lambda i=i: c_unit(512, i))
                    units.append(lambda: colnorm(cq, 3, QL, cqn))
                    for i in range(2):
                        units.append(lambda i=i: c_unit(896, i))
                    units.append(lambda: colnorm(cq, 2, KVL, ckvn))
                    units.append(kr_unit)
                    for i in range(4):
                        units.append(lambda i=i: u_unit(i))
                    for i in range(16):
                        units.append(lambda i=i: g_unit(i))
                    return units

                def h_units(tb):
                    c0, c1 = tb * 512, (tb + 1) * 512
                    cqn, ckvn, krT, cg = cqns[tb % 2], ckvns[tb % 2], krTs[tb % 2], cgs[tb % 2]
                    units = []

                    def q_unit(hh):
                        pq = nextps()
                        for k in range(3):
                            pg.op("pe", lambda e, k=k: e.matmul(pq.t[0:96, :], wq.t[:, k, hh * QK:(hh + 1) * QK], cqn.t[:, k, :],
                                                              start=(k == 0), stop=(k == 2)), reads=[wq, cqn], writes=[pq])
                        psw = nextps()
                        for k in range(3):
                            pg.op("pe", lambda e, k=k: e.matmul(psw.t[0:96, :], wqsw.t[:, k, hh, :], cqn.t[:, k, :],
                                                              start=(k == 0), stop=(k == 2)), reads=[wqsw, cqn], writes=[psw])
                        headfin(hh, pq, psw, None, 0, qT[hh, :, c0:c1], cg)

                    def ksw_unit():
                        psw = nextps()
                        pg.op("pe", lambda e: e.matmul(psw.t[0:96, :], einj.t[:, 1, :], krT.t[:, :], start=True, stop=True),
                              reads=[einj, krT], writes=[psw])
                        pg.op("dve", lambda e: e.tensor_tensor(kswb.t[:, :], psw.t[0:96, :], cg.t[:, 3, :], ALU.mult), reads=[psw, cg], writes=[kswb])

                    def k_unit(hh):
                        pk = nextps()
                        for k in range(2):
                            pg.op("pe", lambda e, k=k: e.matmul(pk.t[0:96, :], wkn.t[:, k, hh, :], ckvn.t[:, k, :],
                                                              start=(k == 0), stop=False), reads=[wkn, ckvn], writes=[pk])
                        pg.op("pe", lambda e: e.matmul(pk.t[0:96, :], einj.t[:, 0, :], krT.t[:, :], start=False, stop=True),
                              reads=[einj, krT], writes=[pk])
                        headfin(hh, pk, None, kswb, 2, kT[hh, :, c0:c1], cg)

                    def v_unit(tt):
                        pv = nextps()
                        for k in range(2):
                            pg.op("pe", lambda e, k=k: e.matmul(pv.t[:, :], ckvn.t[:, k, tt * 128:(tt + 1) * 128],
                                                              wv.t[:, k, :, :].rearrange("p h c -> p (h c)"),
                                                              start=(k == 0), stop=(k == 1)), reads=[ckvn, wv], writes=[pv])
                        v = vt[tt % 2]
                        pg.op("act", lambda e: e.copy(v.t[:, :, 0:VH], pv.t[:, :].rearrange("p (h c) -> p h c", h=H)),
                              reads=[pv], writes=[v])
                        pg.dma("sp", vA[c0 + tt * 128:c0 + (tt + 1) * 128, :, :], v.t[:, :, :], reads=[v])
                    for hh in range(H):
                        units.append(lambda hh=hh: q_unit(hh))
                    units.append(ksw_unit)
                    for hh in range(H):
                        units.append(lambda hh=hh: k_unit(hh))
                    for tt in range(4):
                        units.append(lambda tt=tt: v_unit(tt))
                    return units

                def interleave(a, b):
                    ia = ib = 0
                    while ia < len(a) or ib < len(b):
                        if ia < len(a):
                            a[ia]()
                            ia += 1
                        if ib < len(b):
                            b[ib]()
                            ib += 1

                NBA = NB if KSTOP > 0 else 0
                pend = []
                if NBA > 0:
                    prep(0)
                for tb in range(NBA):
                    interleave(j_units(tb), pend)
                    if tb + 1 < NBA:
                        prep(tb + 1)
                    pend = h_units(tb)
                interleave([], pend)
            pg.barrier()


        def phase_c(l):
            with ExitStack() as es:
                pw2 = [Buf(es.enter_context(nc.psum_tensor("pw%d_%d" % (l, i), [128, 1024], F32)), True) for i in range(3)]
                psc = [Buf(es.enter_context(nc.psum_tensor("pc%d_%d" % (l, i), [128, 512], F32)), True) for i in range(2)]
                vall = pg.sb(es, "vall", [128, NT, H, VH + 1], BF16)
                pg.dma("sp", vall.t[:, :, :, :], vA.rearrange("(t p) h c -> p t h c", p=128), writes=[vall])
                kth = [pg.sb(es, "kth%d" % i, [96, T], BF16) for i in range(2)]
                qtb = [pg.sb(es, "qtb%d" % i, [96, 512], BF16) for i in range(2)]
                pT = [pg.sb(es, "pT%d" % i, [128, 1024], BF16) for i in range(3)]
                scp = [pg.sb(es, "scp%d" % i, [128, 1024], F32) for i in range(2)]
                osb = [pg.sb(es, "osb%d" % i, [VH + 1, 512], F32) for i in range(2)]
                rec = pg.sb(es, "rec", [VH, 512], F32)
                oo = [pg.sb(es, "oo%d" % i, [VH, 512], BF16) for i in range(2)]
                selt = pg.sb(es, "selt", [VH + 1, VH], F32)
                pg.dma("pool", selt.t[:, :], sel_d, writes=[selt])
                it = 0
                for hh in range(H):
                    kt_ = kth[hh % 2]
                    pg.dma("pool", kt_.t[:, :], kT[hh, :, :], writes=[kt_])
                    for qb in range(NB):
                        q_ = qtb[it % 2]
                        pg.dma("sp", q_.t[:, :], qT[hh, :, qb * 512:(qb + 1) * 512], writes=[q_])
                        po = psc[0]
                        pd = psc[1]

                        NP = NT // 2

                        def qk(kp):
                            pb = pw2[kp % 3]
                            for hf_ in range(2):
                                kt = kp * 2 + hf_
                                pg.op("pe", lambda e, kt=kt, hf_=hf_: e.matmul(pb.t[:, hf_ * 512:(hf_ + 1) * 512], kt_.t[:, kt * 128:(kt + 1) * 128], q_.t[:, :],
                                                                            start=True, stop=True), reads=[kt_, q_], writes=[pb])
                        qk(0)
                        if NP > 1:
                            qk(1)
                        for kp in range(NP):
                            pb = pw2[kp % 3]
                            p_ = pT[kp % 3]
                            if kp % 2 == 0 or not CSPLIT:
                                pg.op("act", lambda e, pb=pb, p_=p_: e.activation(p_.t[:, :], pb.t[:, :], AF.Exp), reads=[pb], writes=[p_])
                            else:
                                sc_ = scp[(kp // 2) % 2]
                                pg.op("dve", lambda e, pb=pb, sc_=sc_: e.tensor_copy(sc_.t[:, :], pb.t[:, :]), reads=[pb], writes=[sc_])
                                pg.op("act", lambda e, sc_=sc_, p_=p_: e.activation(p_.t[:, :], sc_.t[:, :], AF.Exp), reads=[sc_], writes=[p_])
                            if kp + 2 < NP:
                                qk(kp + 2)
                            for hf_ in range(2):
                                kt = kp * 2 + hf_
                                pg.op("pe", lambda e, kt=kt, hf_=hf_, p_=p_: e.matmul(po.t[0:VH + 1, :], vall.t[:, kt, hh, :], p_.t[:, hf_ * 512:(hf_ + 1) * 512],
                                                                             start=(kt == 0), stop=(kt == NT - 1)), reads=[vall, p_], writes=[po])
                        o_ = osb[it % 2]
                        pg.op("dve", lambda e, o_=o_: e.tensor_copy(o_.t[:, :], po.t[0:VH + 1, :]), reads=[po], writes=[o_])
                        pg.op("pe", lambda e, o_=o_: e.matmul(pd.t[0:VH, :], selt.t[:, :], o_.t[:, :], start=True, stop=True),
                              reads=[selt, o_], writes=[pd])
                        pg.op("dve", lambda e: e.reciprocal(rec.t[:, :], pd.t[0:VH, :]), reads=[pd], writes=[rec])
                        ob = oo[it % 2]
                        pg.op("dve", lambda e, o_=o_, ob=ob: e.tensor_tensor(ob.t[:, :], o_.t[0:VH, :], rec.t[:, :], ALU.mult),
                              reads=[o_, rec], writes=[ob])
                        pg.dma("sp", ymT[hh * VH:(hh + 1) * VH, qb * 512:(qb + 1) * 512], ob.t[:, :], reads=[ob])
                        it += 1
            pg.barrier()

        def phase_d(l, src, dst):
            with ExitStack() as es:
                psd = [Buf(es.enter_context(nc.psum_tensor("pd%d_%d" % (l, i), [128, 512], F32)), True) for i in range(8)]
                wos = load_weight_bf16(es, "wos", w["w_out_ssm"][l], SW, D, chunk=1024)
                wom = load_weight_bf16(es, "wom", w["w_out_mla"][l], SW, D, chunk=1024)
                wo = load_weight_bf16(es, "wo", w["w_o"][l], D, D, chunk=1024)
                ysb = [pg.sb(es, "ysb%d" % i, [128, 4, 512], BF16) for i in range(2)]
                ymb = [pg.sb(es, "ymb%d" % i, [128, 4, 512], BF16) for i in range(2)]
                gb = [pg.sb(es, "gb%d" % i, [128, 2, 512], BF16) for i in range(2)]
                m0 = [pg.sb(es, "m0%d" % i, [128, 512], F32) for i in range(2)]
                m1 = [pg.sb(es, "m1%d" % i, [128, 512], F32) for i in range(2)]
                mgs = [pg.sb(es, "mg%d" % i, [128, 8, 512], BF16) for i in range(2)]
                xt = [pg.sb(es, "xd%d" % i, [128, 4, D], F32) for i in range(2)]
                cnt = [0]

                def merge(tb):
                    c0, c1 = tb * 512, (tb + 1) * 512
                    ys_, ym_, xb, mg = ysb[tb % 2], ymb[tb % 2], xt[tb % 2], mgs[tb % 2]
                    pg.dma("sp", ys_.t[:, :, :], ysT[:, c0:c1].rearrange("(k p) t -> p k t", p=128), writes=[ys_])
                    pg.dma("pool", ym_.t[:, :, :], ymT[:, c0:c1].rearrange("(k p) t -> p k t", p=128), writes=[ym_])
                    pg.dma("sp", xb.t[:, :, :], src[c0:c1, :].rearrange("(t p) d -> p t d", p=128), writes=[xb])
                    for ft in range(8):
                        n = cnt[0]
                        cnt[0] += 1
                        g_ = gb[n % 2]
                        pg.dma("pool", g_.t[:, :, :], gT[:, c0:c1].rearrange("(a f) t -> f a t", a=2)[ft * 128:(ft + 1) * 128, :, :], writes=[g_])
                        p0, p1 = psd[(2 * n) % 4], psd[(2 * n + 1) % 4]
                        for k in range(4):
                            pg.op("pe", lambda e, k=k, p0=p0: e.matmul(p0.t[:, :], wos.t[:, k, ft * 128:(ft + 1) * 128], ys_.t[:, k, :],
                                                                     start=(k == 0), stop=(k == 3)), reads=[wos, ys_], writes=[p0])
                        for k in range(4):
                            pg.op("pe", lambda e, k=k, p1=p1: e.matmul(p1.t[:, :], wom.t[:, k, ft * 128:(ft + 1) * 128], ym_.t[:, k, :],
                                                                     start=(k == 0), stop=(k == 3)), reads=[wom, ym_], writes=[p1])
                        a0, a1 = m0[n % 2], m1[n % 2]
                        pg.op("dve", lambda e, a0=a0, p0=p0, g_=g_: e.tensor_tensor(a0.t[:, :], p0.t[:, :], g_.t[:, 0, :], ALU.mult), reads=[p0, g_], writes=[a0])
                        pg.op("dve", lambda e, a1=a1, p1=p1, g_=g_: e.tensor_tensor(a1.t[:, :], p1.t[:, :], g_.t[:, 1, :], ALU.mult), reads=[p1, g_], writes=[a1])
                        pg.op("pool", lambda e, a0=a0, a1=a1, mg=mg, ft=ft: e.tensor_tensor(mg.t[:, ft, :], a0.t[:, :], a1.t[:, :], ALU.add), reads=[a0, a1], writes=[mg])

                def outproj(tb):
                    c0, c1 = tb * 512, (tb + 1) * 512
                    xb, mg = xt[tb % 2], mgs[tb % 2]
                    for tt in range(4):
                        for ch in range(2):
                            pb = psd[4 + (tt * 2 + ch) % 4]
                            for k in range(8):
                                pg.op("pe", lambda e, k=k, pb=pb: e.matmul(pb.t[:, :], mg.t[:, k, tt * 128:(tt + 1) * 128], wo.t[:, k, ch * 512:(ch + 1) * 512],
                                                                         start=(k == 0), stop=(k == 7)), reads=[mg, wo], writes=[pb])
                            pg.op("dve", lambda e, pb=pb: e.tensor_tensor(xb.t[:, tt, ch * 512:(ch + 1) * 512], pb.t[:, :],
                                                                         xb.t[:, tt, ch * 512:(ch + 1) * 512], ALU.add), reads=[pb, xb], writes=[xb])
                    pg.dma("sp", dst[c0:c1, :].rearrange("(t p) d -> p t d", p=128), xb.t[:, :, :], reads=[xb])

                merge(0)
                for tb in range(NB):
                    if tb + 1 < NB:
                        merge(tb + 1)
                    outproj(tb)
            pg.barrier()


        def phase_b(l):
            NCH = T // 8
            CH = min(512, NCH)
            NH = NCH // CH
            LV = int(round(math.log2(NCH)))
            assert (1 << LV) == NCH
            TWO_PI = 2.0 * math.pi

            def TT(eng, ob, oap, ab, aap, bb, bap, op):
                pg.op(eng, lambda e: e.tensor_tensor(oap, aap, bap, op), reads=[ab, bb], writes=[ob])

            def TS(eng, ob, oap, ab, aap, s1, s2, op0, op1=None):
                if op1 is None:
                    pg.op(eng, lambda e: e.tensor_scalar(oap, aap, s1, None, op0), reads=[ab], writes=[ob])
                else:
                    pg.op(eng, lambda e: e.tensor_scalar(oap, aap, s1, s2, op0, op1), reads=[ab], writes=[ob])

            with ExitStack() as es:
                psb = [Buf(es.enter_context(nc.psum_tensor("pb%d_%d" % (l, i), [128, 512], F32)), True) for i in range(8)]

                def sm(name, shape=(128, 2, 16), dt=F32):
                    return pg.sb(es, name, list(shape), dt)
                lre, lim, lsb, st, lr, th, mag = [sm(n) for n in ("lre", "lim", "lsb", "st", "lr", "th", "mag")]
                tq, kf, phi, s1, c1, s2, c2, s4, c4, tmpa, tmpb = [sm(n) for n in ("tq", "kf", "phi", "s1", "c1", "s2", "c2", "s4", "c4", "tmpa", "tmpb")]
                ki = sm("ki", dt=I32)
                ar, ai, nr, den, fr, fi = [sm(n) for n in ("ar", "ai", "nr", "den", "fr", "fi")]
                hp = sm("hp", (128, 1))
                pg.op("pool", lambda e: e.memset(hp.t[:, :], math.pi / 2.0), writes=[hp])
                A3 = lambda b: b.t[:, :, :]
                pg.dma("sp", A3(lre), w["ssm_lam_re"][l].rearrange("d (gp two) n -> (two n) d gp", two=2), writes=[lre], allow_slow_non_contiguous=True)
                pg.dma("pool", A3(lim), w["ssm_lam_im"][l].rearrange("d (gp two) n -> (two n) d gp", two=2), writes=[lim], allow_slow_non_contiguous=True)
                lsv = w["ssm_log_step"][l].rearrange("d (gp two) -> two d gp", two=2)
                for two in range(2):
                    pg.dma("sp", lsb.t[two * 64:(two + 1) * 64, :, :], lsv[two].partition_broadcast(64), writes=[lsb], allow_slow_non_contiguous=True)
                pg.op("act", lambda e: e.activation(A3(st), A3(lsb), AF.Exp), reads=[lsb], writes=[st])
                TT("dve", lr, A3(lr), lre, A3(lre), st, A3(st), ALU.mult)
                TT("dve", th, A3(th), lim, A3(lim), st, A3(st), ALU.mult)
                pg.op("act", lambda e: e.activation(A3(mag), A3(lr), AF.Exp), reads=[lr], writes=[mag])
                TS("dve", tq, A3(tq), th, A3(th), 1.0 / TWO_PI, None, ALU.mult)
                pg.op("dve", lambda e: e.tensor_copy(A3(ki), A3(tq)), reads=[tq], writes=[ki])
                pg.op("dve", lambda e: e.tensor_copy(A3(kf), A3(ki)), reads=[ki], writes=[kf])
                pg.op("dve", lambda e: e.scalar_tensor_tensor(A3(phi), A3(kf), -TWO_PI, A3(th), ALU.mult, ALU.add), reads=[kf, th], writes=[phi])
                pg.op("act", lambda e: e.activation(A3(s1), A3(phi), AF.Sin, scale=0.25), reads=[phi], writes=[s1])
                pg.op("act", lambda e: e.activation(A3(c1), A3(phi), AF.Sin, bias=hp.t[:, 0:1], scale=0.25), reads=[phi, hp], writes=[c1])
                TT("dve", tmpa, A3(tmpa), s1, A3(s1), c1, A3(c1), ALU.mult)
                TS("dve", s2, A3(s2), tmpa, A3(tmpa), 2.0, None, ALU.mult)
                TT("dve", tmpb, A3(tmpb), s1, A3(s1), s1, A3(s1), ALU.mult)
                TS("dve", c2, A3(c2), tmpb, A3(tmpb), -2.0, 1.0, ALU.mult, ALU.add)
                TT("dve", tmpa, A3(tmpa), s2, A3(s2), c2, A3(c2), ALU.mult)
                TS("dve", s4, A3(s4), tmpa, A3(tmpa), 2.0, None, ALU.mult)
                TT("dve", tmpb, A3(tmpb), s2, A3(s2), s2, A3(s2), ALU.mult)
                TS("dve", c4, A3(c4), tmpb, A3(tmpb), -2.0, 1.0, ALU.mult, ALU.add)
                TT("dve", ar, A3(ar), mag, A3(mag), c4, A3(c4), ALU.mult)
                TT("dve", ai, A3(ai), mag, A3(mag), s4, A3(s4), ALU.mult)
                TS("dve", nr, A3(nr), ar, A3(ar), -1.0, None, ALU.add)
                TT("dve", tmpa, A3(tmpa), lre, A3(lre), lre, A3(lre), ALU.mult)
                TT("dve", tmpb, A3(tmpb), lim, A3(lim), lim, A3(lim), ALU.mult)
                TT("dve", den, A3(den), tmpa, A3(tmpa), tmpb, A3(tmpb), ALU.add)
                pg.op("dve", lambda e: e.reciprocal(A3(den), A3(den)), reads=[den], writes=[den])
                TT("dve", tmpa, A3(tmpa), nr, A3(nr), lre, A3(lre), ALU.mult)
                TT("dve", tmpb, A3(tmpb), ai, A3(ai), lim, A3(lim), ALU.mult)
                TT("dve", tmpa, A3(tmpa), tmpa, A3(tmpa), tmpb, A3(tmpb), ALU.add)
                TT("dve", fr, A3(fr), tmpa, A3(tmpa), den, A3(den), ALU.mult)
                TT("dve", tmpa, A3(tmpa), ai, A3(ai), lre, A3(lre), ALU.mult)
                TT("dve", tmpb, A3(tmpb), nr, A3(nr), lim, A3(lim), ALU.mult)
                TT("dve", tmpa, A3(tmpa), tmpa, A3(tmpa), tmpb, A3(tmpb), ALU.subtract)
                TT("dve", fi, A3(fi), tmpa, A3(tmpa), den, A3(den), ALU.mult)
                PWr = sm("PWr", (128, 9, 2, 16))
                PWi = sm("PWi", (128, 9, 2, 16))
                ALr = sm("ALr", (128, LV, 2, 16))
                ALi = sm("ALi", (128, LV, 2, 16))
                nALi = sm("nALi", (128, LV, 2, 16))
                pg.op("pool", lambda e: e.memset(PWr.t[:, 0, :, :], 1.0), writes=[PWr])
                pg.op("pool", lambda e: e.memset(PWi.t[:, 0, :, :], 0.0), writes=[PWi])

                def cmul(orb, orap, oib, oiap, xrb, xrap, xib, xiap, yrb, yrap, yib, yiap):
                    TT("dve", tmpa, A3(tmpa), xrb, xrap, yrb, yrap, ALU.mult)
                    TT("dve", tmpb, A3(tmpb), xib, xiap, yib, yiap, ALU.mult)
                    TT("dve", orb, orap, tmpa, A3(tmpa), tmpb, A3(tmpb), ALU.subtract)
                    TT("dve", tmpa, A3(tmpa), xrb, xrap, yib, yiap, ALU.mult)
                    TT("dve", tmpb, A3(tmpb), xib, xiap, yrb, yrap, ALU.mult)
                    TT("dve", oib, oiap, tmpa, A3(tmpa), tmpb, A3(tmpb), ALU.add)
                for k in range(1, 9):
                    cmul(PWr, PWr.t[:, k, :, :], PWi, PWi.t[:, k, :, :], PWr, PWr.t[:, k - 1, :, :], PWi, PWi.t[:, k - 1, :, :],
                         ar, A3(ar), ai, A3(ai))
                pg.op("pool", lambda e: e.tensor_copy(ALr.t[:, 0, :, :], PWr.t[:, 8, :, :]), reads=[PWr], writes=[ALr])
                pg.op("pool", lambda e: e.tensor_copy(ALi.t[:, 0, :, :], PWi.t[:, 8, :, :]), reads=[PWi], writes=[ALi])
                for m in range(1, LV):
                    cmul(ALr, ALr.t[:, m, :, :], ALi, ALi.t[:, m, :, :], ALr, ALr.t[:, m - 1, :, :], ALi, ALi.t[:, m - 1, :, :],
                         ALr, ALr.t[:, m - 1, :, :], ALi, ALi.t[:, m - 1, :, :])
                TS("dve", nALi, nALi.t[:, :, :, :], ALi, ALi.t[:, :, :, :], -1.0, None, ALU.mult)
                bbr = sm("bbr", (128, 2, 16, 32))
                bbi = sm("bbi", (128, 2, 16, 32))
                ctr = sm("ctr", (128, 16, 32))
                cti = sm("cti", (128, 16, 32))
                nctr = sm("nctr", (128, 16, 32))
                ncti = sm("ncti", (128, 16, 32))
                dcol = sm("dcol", (128, 4))
                bmask = sm("bmask", (128, 128))
                pg.dma("sp", bmask.t[:, :], bmask_d, writes=[bmask])
                pg.dma("pool", dcol.t[:, :], w["ssm_d"][l].rearrange("(gt a) p -> (a p) gt", gt=4), writes=[dcol], allow_slow_non_contiguous=True)
                with ExitStack() as es2:
                    bre = pg.sb(es2, "bre", [128, 2, 16, 32], F32)
                    bim = pg.sb(es2, "bim", [128, 2, 16, 32], F32)
                    wt1 = pg.sb(es2, "wt1", [128, 2, 16, 32], F32)
                    wt2 = pg.sb(es2, "wt2", [128, 2, 16, 32], F32)
                    A4 = lambda b: b.t[:, :, :, :]
                    pg.op("pool", lambda e: e.memset(A4(bre), 0.0), writes=[bre])
                    pg.op("pool", lambda e: e.memset(A4(bim), 0.0), writes=[bim])
                    pg.op("pool", lambda e: e.memset(ctr.t[:, :, :], 0.0), writes=[ctr])
                    pg.op("pool", lambda e: e.memset(cti.t[:, :, :], 0.0), writes=[cti])
                    n = 0
                    for two in range(2):
                        for (dst_, nm) in ((bre, "ssm_b_re"), (bim, "ssm_b_im")):
                            srcv = w[nm][l].rearrange("d (gp two) n q -> two n d gp q", two=2)[two]
                            for d_ in range(2):
                                pg.dma("sp" if n % 2 == 0 else "pool", dst_.t[two * 64:(two + 1) * 64, d_, :, two * 16:(two + 1) * 16],
                                       srcv[:, d_, :, :], writes=[dst_], allow_slow_non_contiguous=True)
                                n += 1
                        for (dst_, nm) in ((ctr, "ssm_c_re"), (cti, "ssm_c_im")):
                            srcv = w[nm][l].rearrange("(gp two) p n -> two n gp p", two=2)[two]
                            for g4 in range(16):
                                pg.dma("sp" if n % 2 == 0 else "pool", dst_.t[two * 64:(two + 1) * 64, g4, two * 16:(two + 1) * 16],
                                       srcv[:, g4, :], writes=[dst_], allow_slow_non_contiguous=True)
                                n += 1
                    TS("dve", nctr, nctr.t[:, :, :], ctr, ctr.t[:, :, :], -1.0, None, ALU.mult)
                    TS("dve", ncti, ncti.t[:, :, :], cti, cti.t[:, :, :], -1.0, None, ALU.mult)
                    frb = fr.t[:, :, :].unsqueeze(3).to_broadcast([128, 2, 16, 32])
                    fib = fi.t[:, :, :].unsqueeze(3).to_broadcast([128, 2, 16, 32])
                    TT("dve", wt1, A4(wt1), bre, A4(bre), fr, frb, ALU.mult)
                    TT("dve", wt2, A4(wt2), bim, A4(bim), fi, fib, ALU.mult)
                    TT("dve", bbr, A4(bbr), wt1, A4(wt1), wt2, A4(wt2), ALU.subtract)
                    TT("dve", wt1, A4(wt1), bim, A4(bim), fr, frb, ALU.mult)
                    TT("dve", wt2, A4(wt2), bre, A4(bre), fi, fib, ALU.mult)
                    TT("dve", bbi, A4(bbi), wt1, A4(wt1), wt2, A4(wt2), ALU.add)
                    pg.barrier()
                nc.leave_named_scope() if False else None
                TEr = sm("TEr", (128, 8, 2, 4, 32))
                TEi = sm("TEi", (128, 8, 2, 4, 32))
                wa = sm("wa", (128, 2, 4, 32))
                wb = sm("wb", (128, 2, 4, 32))
                WS = sm("WS", (128, 2, 2, 8, 128), BF16)
                WY = sm("WY", (128, 2, 2, 8, 4, 32), BF16)
                ya = sm("ya", (128, 4, 32))
                yb = sm("yb", (128, 4, 32))
                Km = sm("Km", (128, 15, 128), BF16)
                ktmp = sm("ktmp", (128, 128))
                usb = sm("usb", (128, T), BF16)
                usd = sm("usd", (128, 8, NCH), BF16)
                X16 = sm("X16", (128, 2, 4, 2, NCH + 1), BF16)
                Xm_t = sm("Xm", (128, 4, 2, NCH))
                XA = [Buf(Xm_t.t) for _ in range(4)]
                ysb_ = sm("ysbf", (128, CH * 8))
                g1 = [sm("g1%d" % i, (128, 1024)) for i in range(2)]
                g2 = [sm("g2%d" % i, (128, 1024)) for i in range(2)]
                ygb = [sm("ygb%d" % i, (128, 1024), BF16) for i in range(2)]
                for gt in range(4):
                    gs = slice(gt * 4, gt * 4 + 4)
                    pg.dma("sp", usb.t[:, :], uT[gt * 128:(gt + 1) * 128, :], writes=[usb])
                    hN = NCH // 2
                    pg.op("act", lambda e: e.copy(usd.t[:, :, 0:hN], usb.t[:, 0:hN * 8].rearrange("p (c j) -> p j c", j=8)), reads=[usb], writes=[usd])
                    pg.op("pool", lambda e: e.tensor_copy(usd.t[:, :, hN:NCH], usb.t[:, hN * 8:NCH * 8].rearrange("p (c j) -> p j c", j=8)), reads=[usb], writes=[usd])
                    A4w = lambda b: b.t[:, :, :, :]
                    for e_ in range(8):
                        prb = PWr.t[:, e_, :, gs].unsqueeze(3).to_broadcast([128, 2, 4, 32])
                        pib = PWi.t[:, e_, :, gs].unsqueeze(3).to_broadcast([128, 2, 4, 32])
                        TT("dve", wa, A4w(wa), bbr, bbr.t[:, :, gs, :], PWr, prb, ALU.mult)
                        TT("dve", wb, A4w(wb), bbi, bbi.t[:, :, gs, :], PWi, pib, ALU.mult)
                        TT("dve", TEr, TEr.t[:, e_, :, :, :], wa, A4w(wa), wb, A4w(wb), ALU.subtract)
                        TT("dve", wa, A4w(wa), bbi, bbi.t[:, :, gs, :], PWr, prb, ALU.mult)
                        TT("dve", wb, A4w(wb), bbr, bbr.t[:, :, gs, :], PWi, pib, ALU.mult)
                        TT("dve", TEi, TEi.t[:, e_, :, :, :], wa, A4w(wa), wb, A4w(wb), ALU.add)
                    n = 0
                    for ri, TE in ((0, TEr), (1, TEi)):
                        for d_ in range(2):
                            for j0 in (0, 4):
                                pb = psb[n % 2]
                                for jj in range(4):
                                    j = j0 + jj
                                    e_ = 7 - j if d_ == 0 else j
                                    pg.op("pe", lambda e, pb=pb, jj=jj, e_=e_, d_=d_, TE=TE: e.transpose(
                                        pb.t[:, jj * 128:(jj + 1) * 128], TE.t[:, e_, d_, :, :].rearrange("p g c -> p (g c)"), idf.t[:, :]),
                                        reads=[TE, idf], writes=[pb])
                                pg.op("act", lambda e, pb=pb, ri=ri, d_=d_, j0=j0: e.copy(
                                    WS.t[:, ri, d_, j0:j0 + 4, :].rearrange("p j c -> p (j c)"), pb.t[:, :]), reads=[pb], writes=[WS])
                                n += 1
                    for d_ in range(2):
                        for i in range(8):
                            pw = i + 1 if d_ == 0 else 8 - i
                            prb = PWr.t[:, pw, d_, gs].unsqueeze(2).to_broadcast([128, 4, 32])
                            pib = PWi.t[:, pw, d_, gs].unsqueeze(2).to_broadcast([128, 4, 32])
                            A3y = lambda b: b.t[:, :, :]
                            TT("dve", ya, A3y(ya), ctr, ctr.t[:, gs, :], PWr, prb, ALU.mult)
                            TT("dve", yb, A3y(yb), ncti, ncti.t[:, gs, :], PWi, pib, ALU.mult)
                            TT("dve", WY, WY.t[:, d_, 0, i, :, :], ya, A3y(ya), yb, A3y(yb), ALU.add)
                            TT("dve", ya, A3y(ya), nctr, nctr.t[:, gs, :], PWi, pib, ALU.mult)
                            TT("dve", yb, A3y(yb), ncti, ncti.t[:, gs, :], PWr, prb, ALU.mult)
                            TT("dve", WY, WY.t[:, d_, 1, i, :, :], ya, A3y(ya), yb, A3y(yb), ALU.add)
                    ctr_v = ctr.t[:, gs, :].rearrange("p g c -> p (g c)")
                    ncti_v = ncti.t[:, gs, :].rearrange("p g c -> p (g c)")
                    for ti in range(15):
                        tau = ti - 7
                        lst = [(0, tau)] if tau > 0 else ([(1, -tau)] if tau < 0 else [(0, 0), (1, 0)])
                        pk = psb[6 + ti % 2]
                        nm_ = len(lst) * 2
                        q_ = 0
                        for (d_, e_) in lst:
                            for (TE, cv, cb) in ((TEr, ctr_v, ctr), (TEi, ncti_v, ncti)):
                                pg.op("pe", lambda e, TE=TE, cv=cv, d_=d_, e_=e_, q_=q_, pk=pk: e.matmul(
                                    pk.t[:, 0:128], TE.t[:, e_, d_, :, :].rearrange("p g c -> p (g c)"), cv,
                                    start=(q_ == 0), stop=(q_ == nm_ - 1)), reads=[TE, cb], writes=[pk])
                                q_ += 1
                        if tau != 0:
                            pg.op("dve", lambda e, pk=pk, ti=ti: e.tensor_tensor(Km.t[:, ti, :], pk.t[:, 0:128], bmask.t[:, :], ALU.mult),
                                  reads=[pk, bmask], writes=[Km])
                        else:
                            pg.op("dve", lambda e, pk=pk: e.tensor_tensor(ktmp.t[:, :], pk.t[:, 0:128], bmask.t[:, :], ALU.mult),
                                  reads=[pk, bmask], writes=[ktmp])
                            pg.op("dve", lambda e, ti=ti: e.scalar_tensor_tensor(Km.t[:, ti, :], idf.t[:, :], dcol.t[:, gt:gt + 1], ktmp.t[:, :],
                                                                                ALU.mult, ALU.add), reads=[idf, dcol, ktmp], writes=[Km])
                    pg.op("pool", lambda e: e.memset(X16.t[:, 0, :, :, 0:1], 0.0), writes=[X16])
                    pg.op("pool", lambda e: e.memset(X16.t[:, 1, :, :, NCH:NCH + 1], 0.0), writes=[X16])
                    if True:
                        n = 0
                        for d_ in range(2):
                            for pp in range(4):
                                for ri in range(2):
                                    for hf in range(NH):
                                        pb = psb[2 + n % 2]
                                        n += 1
                                        for j in range(8):
                                            st0 = hf * CH * 8 + j
                                            pg.op("pe", lambda e, pb=pb, pp=pp, ri=ri, d_=d_, j=j, hf=hf: e.matmul(
                                                pb.t[:, 0:CH], WS.t[32 * pp:32 * pp + 32, ri, d_, j, :],
                                                usd.t[32 * pp:32 * pp + 32, j, hf * CH:(hf + 1) * CH],
                                                start=(j == 0), stop=(j == 7), tile_position=(32 * pp, 0)),
                                                reads=[WS, usd], writes=[pb])
                                        pg.op("act", lambda e, pb=pb, pp=pp, ri=ri, hf=hf: e.copy(
                                            XA[pp].t[:, pp, ri, hf * CH:(hf + 1) * CH], pb.t[:, 0:CH]), reads=[pb], writes=[XA[pp]])
                            for pp in range(4):
                                gp = gt * 4 + pp
                                X_ = XA[pp]
                                sched = [(m, (1 << (m + 1)) - 1, NCH >> (m + 1)) for m in range(LV)]
                                sched += [(m, (1 << (m + 1)) + (1 << m) - 1, (NCH >> (m + 1)) - 1) for m in range(LV - 2, -1, -1)]
                                for (m, p0, cnt) in sched:
                                    if cnt <= 0:
                                        continue
                                    strd = 1 << (m + 1)
                                    sft = 1 << m
                                    a_r = ALr.t[:, m, d_, gp:gp + 1]
                                    a_i = ALi.t[:, m, d_, gp:gp + 1]
                                    na_i = nALi.t[:, m, d_, gp:gp + 1]
                                    if d_ == 0:
                                        d0 = p0
                                        s0 = p0 - sft
                                    else:
                                        d0 = NCH - 1 - (p0 + (cnt - 1) * strd)
                                        s0 = d0 + sft
                                    dsl = slice(d0, d0 + (cnt - 1) * strd + 1, strd)
                                    ssl = slice(s0, s0 + (cnt - 1) * strd + 1, strd)
                                    for (ori, sri, sc) in ((0, 0, a_r), (0, 1, na_i), (1, 0, a_i), (1, 1, a_r)):
                                        pg.op("dve", lambda e, ori=ori, sri=sri, sc=sc, dsl=dsl, ssl=ssl, pp=pp, X_=X_: e.scalar_tensor_tensor(
                                            X_.t[:, pp, ori, dsl], X_.t[:, pp, sri, ssl], sc, X_.t[:, pp, ori, dsl], ALU.mult, ALU.add),
                                            reads=[X_, ALr, ALi, nALi], writes=[X_])
                                off = 1 if d_ == 0 else 0
                                pg.op("act", lambda e, X_=X_, pp=pp, d_=d_, off=off: e.copy(X16.t[:, d_, pp, :, off:off + NCH], X_.t[:, pp, :, :]),
                                      reads=[X_], writes=[X16])
                    if True:
                        n = 0
                        for hf in range(NH):
                            for i in range(8):
                                py = psb[4 + n % 2]
                                n += 1
                                for j in range(8):
                                    st0 = hf * CH * 8 + j
                                    pg.op("pe", lambda e, py=py, i=i, j=j, hf=hf: e.matmul(
                                        py.t[:, 0:CH], Km.t[:, (i - j) + 7, :], usd.t[:, j, hf * CH:(hf + 1) * CH],
                                        start=(j == 0), stop=False), reads=[Km, usd], writes=[py])
                                q_ = 0
                                for pp in range(4):
                                    for d_ in range(2):
                                        for ri in range(2):
                                            off = hf * CH + (0 if d_ == 0 else 1)
                                            pg.op("pe", lambda e, py=py, pp=pp, d_=d_, ri=ri, off=off, i=i, q_=q_: e.matmul(
                                                py.t[32 * pp:32 * pp + 32, 0:CH], WY.t[:, d_, ri, i, pp, :], X16.t[:, d_, pp, ri, off:off + CH],
                                                start=False, stop=(d_ == 1 and ri == 1), tile_position=(0, 32 * pp)), reads=[WY, X16], writes=[py])
                                            q_ += 1
                                pg.op("act", lambda e, py=py, i=i: e.copy(ysb_.t[:, i:i + (CH - 1) * 8 + 1:8], py.t[:, 0:CH]), reads=[py], writes=[ysb_])
                            for cc in range(0, CH * 8, 1024):
                                cw = min(1024, CH * 8 - cc)
                                k_ = (cc // 1024) % 2
                                yv = ysb_.t[:, cc:cc + cw]
                                a_, b_, o_ = g1[k_], g2[k_], ygb[k_]
                                pg.op("pool", lambda e, a_=a_, yv=yv, cw=cw: e.tensor_tensor(a_.t[:, 0:cw], yv, yv, ALU.mult), reads=[ysb_], writes=[a_])
                                pg.op("pool", lambda e, a_=a_, cw=cw: e.tensor_scalar(a_.t[:, 0:cw], a_.t[:, 0:cw], 0.044715, 1.0, ALU.mult, ALU.add),
                                      reads=[a_], writes=[a_])
                                pg.op("pool", lambda e, a_=a_, yv=yv, cw=cw: e.tensor_tensor(a_.t[:, 0:cw], a_.t[:, 0:cw], yv, ALU.mult), reads=[a_, ysb_], writes=[a_])
                                pg.op("act", lambda e, a_=a_, b_=b_, cw=cw: e.activation(b_.t[:, 0:cw], a_.t[:, 0:cw], AF.Sigmoid, scale=1.5957691216057308),
                                      reads=[a_], writes=[b_])
                                pg.op("dve", lambda e, b_=b_, o_=o_, yv=yv, cw=cw: e.tensor_tensor(o_.t[:, 0:cw], b_.t[:, 0:cw], yv, ALU.mult),
                                      reads=[b_, ysb_], writes=[o_])
                                t0_ = hf * CH * 8 + cc
                                pg.dma("sp", ygT[gt * 128:(gt + 1) * 128, t0_:t0_ + cw], o_.t[:, 0:cw], reads=[o_])
            pg.barrier()
            with ExitStack() as es:
                psg = [Buf(es.enter_context(nc.psum_tensor("pg%d_%d" % (l, i), [128, 512], F32)), True) for i in range(4)]
                wglu = load_weight_bf16(es, "wglu", w["w_glu"][l], SW, SW, chunk=512)
                bgl = pg.sb(es, "bgl", [128, 4], F32)
                pg.dma("pool", bgl.t[:, :], w["b_glu"][l].rearrange("(k p) -> p k", p=128), writes=[bgl], allow_slow_non_contiguous=True)
                ygi = [pg.sb(es, "ygi%d" % i, [128, 4, 512], BF16) for i in range(2)]
                sgb = [pg.sb(es, "sgb%d" % i, [128, 512], F32) for i in range(2)]
                yo = [pg.sb(es, "yo%d" % i, [128, 512], BF16) for i in range(2)]
                n = 0
                for tb in range(NB):
                    c0, c1 = tb * 512, (tb + 1) * 512
                    yg_ = ygi[tb % 2]
                    pg.dma("sp", yg_.t[:, :, :], ygT[:, c0:c1].rearrange("(k p) t -> p k t", p=128), writes=[yg_])
                    for ft in range(4):
                        pb = psg[n % 4]
                        for k in range(4):
                            pg.op("pe", lambda e, k=k, pb=pb, ft=ft: e.matmul(pb.t[:, :], wglu.t[:, k, ft * 128:(ft + 1) * 128], yg_.t[:, k, :],
                                                                            start=(k == 0), stop=(k == 3)), reads=[wglu, yg_], writes=[pb])
                        sg_, y_ = sgb[n % 2], yo[n % 2]
                        pg.op("act", lambda e, pb=pb, sg_=sg_, ft=ft: e.activation(sg_.t[:, :], pb.t[:, :], AF.Sigmoid, bias=bgl.t[:, ft:ft + 1]),
                              reads=[pb, bgl], writes=[sg_])
                        pg.op("dve", lambda e, sg_=sg_, y_=y_, ft=ft: e.tensor_tensor(y_.t[:, :], sg_.t[:, :], yg_.t[:, ft, :], ALU.mult),
                              reads=[sg_, yg_], writes=[y_])
                        pg.dma("pool", ysT[ft * 128:(ft + 1) * 128, c0:c1], y_.t[:, :], reads=[y_])
                        n += 1
            pg.barrier()

        src = x_in
        for l in range(nlayers):
            dst = y_out if l == nlayers - 1 else xs
            if "a" in phases:
                with nc.named_scope("A%d" % l):
                    phase_a(l, src)
            if "b" in phases:
                with nc.named_scope("B%d" % l):
                    phase_b(l)
            if "c" in phases:
                with nc.named_scope("C%d" % l):
                    phase_c(l)
            if "d" in phases:
                with nc.named_scope("D%d" % l):
                    phase_d(l, src, x1)
            if "f" in phases:
                with nc.named_scope("F%d" % l):
                    phase_ffn(l, x1 if "d" in phases else src, dst)
            src = dst
        pg.barrier()
    return nc


_CACHE = {}
RUN_KW = {}


def _consts(T):
    half = ROPE // 2
    inv = (10000.0 ** (-np.arange(half, dtype=np.float32) / half)).astype(np.float32)
    ang = np.arange(T, dtype=np.float32)[:, None] * inv[None, :]
    cos = np.cos(ang).astype(np.float32).T
    sin = np.sin(ang).astype(np.float32).T
    cs = np.zeros((2, QK, T), np.float32)
    cs[0, :NOPE] = 1.0
    cs[0, NOPE:NOPE + half] = cos
    cs[0, NOPE + half:] = cos
    cs[1, NOPE:NOPE + half] = -sin
    cs[1, NOPE + half:] = sin
    e = np.zeros((32, 2, QK), np.float32)
    for i in range(32):
        e[i, 0, NOPE + i] = 1.0
        e[i, 1, NOPE + (i + 16) % 32] = 1.0
    return {
        "cs_tab": cs,
        "bmask": np.kron(np.eye(4, dtype=np.float32), np.ones((32, 32), np.float32)),
        "sel": np.concatenate([np.zeros((VH, VH), np.float32), np.ones((1, VH), np.float32)], axis=0),
        "ones_b": np.ones((128, 128), np.float32).astype(ml_dtypes.bfloat16),
        "einj": e.astype(ml_dtypes.bfloat16),
        "ident_f": np.eye(128, dtype=np.float32),
        "ident_b": np.eye(128, dtype=np.float32).astype(ml_dtypes.bfloat16),
    }


def run(inputs, T, nlayers, ncores, debug=False, phases="abcdf"):
    key = (T, nlayers, debug, phases)
    if key not in _CACHE:
        _CACHE[key] = build(T, nlayers, debug, phases)
    nc = _CACHE[key]
    base = {k: np.ascontiguousarray(np.asarray(v, dtype=np.float32)) for k, v in inputs.items() if k != "x"}
    base.update(_consts(T))
    x = np.asarray(inputs["x"], dtype=np.float32)
    in_maps = []
    for c in range(ncores):
        m = dict(base)
        m["x"] = np.ascontiguousarray(x[c % x.shape[0], :T, :])
        in_maps.append(m)
    c0 = int(os.environ.get('KCORE', '0'))
    res = run_bass_kernel_spmd(nc, in_maps, core_ids=list(range(c0, c0 + ncores)), **RUN_KW)
    return res


def kernel(**inputs):
    x = np.asarray(inputs["x"])
    B, T, _ = x.shape
    res = run(inputs, T, DEPTH, 8)
    out = np.stack([np.asarray(res.results[b]["y"], dtype=np.float32) for b in range(B)], axis=0)
    return out.astype(np.float32)
```
